# Optimizing a Trainium2 kernel written in Bass

```python
import jax, jax.numpy as jnp
from jax import lax
import numpy as np

D_MODEL = 4096
BATCH = 4
SEQ = 4096
DEPTH = 4

GRID_W = 64
HEAD_DIM = 128
ATTN_WIDTH = D_MODEL // 2
ATTN_HEADS = ATTN_WIDTH // HEAD_DIM
ATTN_KV_HEADS = ATTN_HEADS // 4
ATTN_GROUP = ATTN_HEADS // ATTN_KV_HEADS
KV_WIDTH = ATTN_KV_HEADS * HEAD_DIM
Q_BLOCK = 128
RET_WIDTH = D_MODEL - ATTN_WIDTH
RET_V_DIM = 256
RET_HEADS = RET_WIDTH // RET_V_DIM
RET_QK_DIM = 128
RET_QK_WIDTH = RET_HEADS * RET_QK_DIM
RET_CHUNK = 128
RET_DECAY_BASE_EXP = 5
MIX_WIDTH = ATTN_WIDTH + RET_WIDTH
ROPE_THETA = 10000.0
EPS = 1e-6

IN_SPLITS = (ATTN_WIDTH, KV_WIDTH, KV_WIDTH, ATTN_WIDTH, RET_QK_WIDTH, RET_QK_WIDTH, RET_WIDTH, RET_WIDTH)
IN_WIDTH = sum(IN_SPLITS)
SPLIT_POINTS = tuple(int(p) for p in np.cumsum(IN_SPLITS)[:-1])

kernel_name = 'hymba_gqa_axialrope_biretention_encoder'


def rmsnorm(x, g):
    xf = x.astype(jnp.float32)
    y = xf * lax.rsqrt(jnp.mean(xf * xf, axis=-1, keepdims=True) + EPS)
    return (y * g.astype(jnp.float32)).astype(x.dtype)


def axial_rope_tables(seq_len):
    rows = seq_len // GRID_W
    row = jnp.repeat(jnp.arange(rows), GRID_W).astype(jnp.float32)
    col = jnp.tile(jnp.arange(GRID_W), rows).astype(jnp.float32)
    axis_dim = HEAD_DIM // 2
    inv = ROPE_THETA ** (-jnp.arange(0, axis_dim, 2, dtype=jnp.float32) / axis_dim)
    ang_r = row[:, None] * inv[None, :]
    ang_c = col[:, None] * inv[None, :]
    return (jnp.cos(ang_r), jnp.sin(ang_r), jnp.cos(ang_c), jnp.sin(ang_c))


def _rotate(xp, cos, sin):
    x1, x2 = jnp.split(xp, 2, axis=-1)
    c = cos[None, :, None, :]
    s = sin[None, :, None, :]
    return jnp.concatenate([x1 * c - x2 * s, x1 * s + x2 * c], axis=-1)


def apply_axial_rope(x, rope):
    cos_r, sin_r, cos_c, sin_c = rope
    xf = x.astype(jnp.float32)
    half = x.shape[-1] // 2
    out = jnp.concatenate([_rotate(xf[..., :half], cos_r, sin_r),
                           _rotate(xf[..., half:], cos_c, sin_c)], axis=-1)
    return out.astype(x.dtype)


def gqa_attention(q, k, v):
    B, S = q.shape[0], q.shape[1]
    nb = S // Q_BLOCK
    qb = q.reshape(B, nb, Q_BLOCK, ATTN_KV_HEADS, ATTN_GROUP, HEAD_DIM).transpose(1, 0, 3, 4, 2, 5)
    kt = k.transpose(0, 2, 1, 3)
    vt = v.transpose(0, 2, 1, 3)

    def block(qblk):
        s = jnp.einsum('bkgqd,bksd->bkgqs', qblk, kt).astype(jnp.float32)
        p = jax.nn.softmax(s, axis=-1).astype(vt.dtype)
        return jnp.einsum('bkgqs,bksd->bkgqd', p, vt)

    o = lax.map(block, qb)
    return o.transpose(1, 0, 4, 2, 3, 5).reshape(B, S, ATTN_WIDTH)


def retention_chunkwise(q, k, v, log_g, include_diag):
    q = q.astype(jnp.float32)
    k = k.astype(jnp.float32)
    v = v.astype(jnp.float32)
    log_g = log_g.astype(jnp.float32)
    B, H, S, dk = q.shape
    dv = v.shape[-1]
    C = RET_CHUNK
    n = S // C
    qc = q.reshape(B, H, n, C, dk).transpose(2, 0, 1, 3, 4)
    kc = k.reshape(B, H, n, C, dk).transpose(2, 0, 1, 3, 4)
    vc = v.reshape(B, H, n, C, dv).transpose(2, 0, 1, 3, 4)
    idx = jnp.arange(C, dtype=jnp.float32)
    diff = idx[:, None] - idx[None, :]
    mask = (diff >= 0) if include_diag else (diff > 0)
    safe = jnp.where(mask, diff, 0.0)
    decay_in = jnp.where(mask[None], jnp.exp(log_g[:, None, None] * safe[None]), 0.0)
    xi = jnp.exp(log_g[:, None] * (idx + 1.0)[None])[..., None]
    zeta = jnp.exp(log_g[:, None] * (C - 1.0 - idx)[None])[..., None]
    g_chunk = jnp.exp(log_g * C)[:, None, None]

    def step(state, inp):
        qj, kj, vj = inp
        scores = jnp.einsum('bhqd,bhkd->bhqk', qj, kj) * decay_in
        inner = jnp.einsum('bhqk,bhke->bhqe', scores, vj)
        cross = jnp.einsum('bhqd,bhde->bhqe', qj * xi, state)
        state = g_chunk * state + jnp.einsum('bhkd,bhke->bhde', kj * zeta, vj)
        return state, inner + cross

    state0 = jnp.zeros((B, H, dk, dv), jnp.float32)
    _, out = lax.scan(step, state0, (qc, kc, vc))
    return out.transpose(1, 2, 0, 3, 4).reshape(B, H, S, dv)


def bidirectional_retention(q, k, v, log_gf, log_gb):
    fwd = retention_chunkwise(q, k, v, log_gf, True)
    bwd = retention_chunkwise(jnp.flip(q, 2), jnp.flip(k, 2), jnp.flip(v, 2), log_gb, False)
    return fwd + jnp.flip(bwd, 2)


def setup_inputs(seed: int = 0) -> dict:
    key = jax.random.key(seed)
    ks = jax.random.split(key, 12)
    x = jax.random.normal(ks[0], (BATCH, SEQ, D_MODEL), jnp.float32)
    norm_w = 1.0 + 0.02 * jax.random.normal(ks[1], (DEPTH, D_MODEL), jnp.float32)
    w_in = jax.random.normal(ks[2], (DEPTH, D_MODEL, IN_WIDTH), jnp.float32) * (D_MODEL ** -0.5)
    q_norm = 1.0 + 0.02 * jax.random.normal(ks[3], (DEPTH, HEAD_DIM), jnp.float32)
    k_norm = 1.0 + 0.02 * jax.random.normal(ks[4], (DEPTH, HEAD_DIM), jnp.float32)
    base = jnp.asarray(np.log(2.0 ** (RET_DECAY_BASE_EXP + np.arange(RET_HEADS)) - 1.0), jnp.float32)
    ret_decay_fwd = base[None] + 0.1 * jax.random.normal(ks[5], (DEPTH, RET_HEADS), jnp.float32)
    ret_decay_bwd = base[None] + 0.1 * jax.random.normal(ks[6], (DEPTH, RET_HEADS), jnp.float32)
    ret_norm = 1.0 + 0.02 * jax.random.normal(ks[7], (DEPTH, RET_HEADS, RET_V_DIM), jnp.float32)
    w_out = jax.random.normal(ks[8], (DEPTH, MIX_WIDTH, D_MODEL), jnp.float32) * (MIX_WIDTH ** -0.5)
    final_norm = 1.0 + 0.02 * jax.random.normal(ks[9], (D_MODEL,), jnp.float32)
    return {'x': x, 'norm_w': norm_w, 'w_in': w_in, 'q_norm': q_norm, 'k_norm': k_norm,
            'ret_decay_fwd': ret_decay_fwd, 'ret_decay_bwd': ret_decay_bwd, 'ret_norm': ret_norm,
            'w_out': w_out, 'final_norm': final_norm}


def reference(x, norm_w, w_in, q_norm, k_norm, ret_decay_fwd, ret_decay_bwd, ret_norm, w_out, final_norm):
    B, S, _ = x.shape
    rope = axial_rope_tables(S)
    attn_scale = HEAD_DIM ** -0.5
    ret_scale = RET_QK_DIM ** -0.5
    for l in range(DEPTH):
        h = rmsnorm(x, norm_w[l])
        proj = h @ w_in[l]
        aq, ak, av, ag, rq, rk, rv, rg = jnp.split(proj, SPLIT_POINTS, axis=-1)

        aq = apply_axial_rope(rmsnorm(aq.reshape(B, S, ATTN_HEADS, HEAD_DIM), q_norm[l]), rope) * attn_scale
        ak = apply_axial_rope(rmsnorm(ak.reshape(B, S, ATTN_KV_HEADS, HEAD_DIM), k_norm[l]), rope)
        av = av.reshape(B, S, ATTN_KV_HEADS, HEAD_DIM)
        a_out = (jax.nn.silu(ag) * gqa_attention(aq, ak, av)).astype(x.dtype)

        rq = apply_axial_rope(rq.reshape(B, S, RET_HEADS, RET_QK_DIM), rope).transpose(0, 2, 1, 3)
        rk = (apply_axial_rope(rk.reshape(B, S, RET_HEADS, RET_QK_DIM), rope) * ret_scale).transpose(0, 2, 1, 3)
        rv = rv.reshape(B, S, RET_HEADS, RET_V_DIM).transpose(0, 2, 1, 3)
        log_gf = jax.nn.log_sigmoid(ret_decay_fwd[l].astype(jnp.float32))
        log_gb = jax.nn.log_sigmoid(ret_decay_bwd[l].astype(jnp.float32))
        r = bidirectional_retention(rq, rk, rv, log_gf, log_gb).transpose(0, 2, 1, 3)
        r = rmsnorm(r, ret_norm[l]).reshape(B, S, RET_WIDTH)
        r_out = (jax.nn.silu(rg.astype(jnp.float32)) * r).astype(x.dtype)

        x = x + jnp.concatenate([a_out, r_out], axis=-1) @ w_out[l]
    return rmsnorm(x, final_norm)
```

```python
import numpy as np
from contextlib import ExitStack
import concourse.bass as bass
import concourse.mybir as mybir
from concourse.bass_utils import run_bass_kernel_spmd

F32 = mybir.dt.float32
BF16 = mybir.dt.bfloat16
AF = mybir.ActivationFunctionType
ALU = mybir.AluOpType
AX = mybir.AxisListType

D = 4096
KC = 32
INW = 11264
EPS = 1e-6
NDS = 24

BLOCK_ORDER = [("aq", 0), ("aq", 1), ("aq", 2), ("aq", 3), ("ak", 4), ("av", 5),
               ("ag", 6), ("ag", 7), ("ag", 8), ("ag", 9), ("rq", 10), ("rq", 11),
               ("rv", 14), ("rv", 15), ("rv", 16), ("rv", 17), ("rk", 12), ("rk", 13),
               ("rg", 18), ("rg", 19), ("rg", 20), ("rg", 21)]


class T:
    __slots__ = ("name", "w", "r", "ex")

    def __init__(self, name="", ex=False):
        self.name = name
        self.ex = ex
        self.w = None
        self.r = []


class K:
    def __init__(self, nc, es):
        self.nc = nc
        self.E = {"pe": nc.tensor, "act": nc.scalar, "dve": nc.vector, "pool": nc.gpsimd, "sp": nc.sync}
        self.sems = {}
        self.cnt = {}
        for e in ("pe", "act", "dve", "pool"):
            self.sems[e] = es.enter_context(nc.semaphore("p_" + e))
            self.cnt[e] = 0
        for q in ("sp", "pool"):
            for i in range(NDS):
                self.sems[(q, i)] = es.enter_context(nc.semaphore("d%s%d" % (q, i)))
                self.cnt[(q, i)] = 0
        self.dnext = {"sp": 0, "pool": 0}
        self.waited = {e: {} for e in self.E}
        self.n_inst = 0

    def _wait(self, eng, ev):
        if ev is None:
            return
        key, val = ev
        if key == eng and eng == "pe":
            return
        if self.waited[eng].get(key, 0) >= val:
            return
        self.waited[eng][key] = val
        self.E[eng].wait_ge(self.sems[key], val)
        self.n_inst += 1

    def _deps(self, eng, reads, writes):
        for t in reads:
            self._wait(eng, t.w)
        for t in writes:
            self._wait(eng, t.w)
            for ev in t.r:
                self._wait(eng, ev)

    def _record(self, ev, reads, writes):
        for t in reads:
            t.r.append(ev)
            if len(t.r) > 48:
                best = {}
                for k_, v_ in t.r:
                    if best.get(k_, 0) < v_:
                        best[k_] = v_
                t.r = list(best.items())
        for t in writes:
            t.w = ev
            t.r = []

    def op(self, eng, fn, reads=(), writes=(), inc=True):
        exr = [t for t in reads if t.ex]
        if exr:
            writes = list(writes) + [t for t in exr if t not in writes]
            reads = [t for t in reads if not t.ex]
        self._deps(eng, reads, writes)
        ins = fn(self.E[eng])
        self.n_inst += 1
        if inc:
            self.cnt[eng] += 1
            ins.then_inc(self.sems[eng], 1)
            ev = (eng, self.cnt[eng])
        else:
            ev = (eng, self.cnt[eng] + 1)
        self._record(ev, reads, writes)
        return ev

    def dma(self, q, out, in_, reads=(), writes=(), **kw):
        self._deps(q, reads, writes)
        i = self.dnext[q]
        self.dnext[q] = (i + 1) % NDS
        key = (q, i)
        self._wait(q, (key, self.cnt[key]))
        ins = self.E[q].dma_start(out=out, in_=in_, **kw)
        self.n_inst += 1
        self.cnt[key] += 16
        ins.then_inc(self.sems[key], 16)
        ev = (key, self.cnt[key])
        self._record(ev, reads, writes)
        return ev

    def barrier(self):
        for eng in self.E:
            for key, c in self.cnt.items():
                if c > 0:
                    self._wait(eng, (key, c))


def fap(base, pat):
    a = base.ap
    return bass.AP(base.tensor, base.offset, [list(a[0])] + [list(p) for p in pat])


import os
STOP = os.environ.get("KSTOP", "")


def build(L, TOK, NP, x_from_input=True):
    NT = TOK // 128
    NTB = TOK // 512
    SF = TOK * NP
    NKT = SF // 128
    nc = bass.Bass("TRN2", target_bir_lowering=False)

    def din(name, shape, dt=F32):
        return nc.dram_tensor(name, shape, dt, kind="ExternalInput")

    x_in = din("x", [TOK, D]).ap()
    w_in = din("w_in", [L, D, INW])
    w_out = din("w_out", [L, D, D])
    norm_w = din("norm_w", [L, D])
    q_norm = din("q_norm", [L, 128])
    k_norm = din("k_norm", [L, 128])
    rdec = din("rdec", [L, 16])
    ret_norm = din("ret_norm", [L, 2048])
    final_norm = din("final_norm", [1, D])
    rope_d = din("rope", [TOK, 256]).ap()
    cst_d = din("cst", [128, 1024]).ap()
    out_d = nc.dram_tensor("out", [TOK, D], F32, kind="ExternalOutput").ap()

    def dsc(name, shape, dt):
        return nc.dram_tensor(name, shape, dt).ap()

    wibs = [dsc("wib%d" % l_, [22 * 128, 16384], BF16) for l_ in range(L)]
    wobs = [dsc("wob%d" % l_, [8 * 128, 16384], BF16) for l_ in range(L)]
    xres = dsc("xres", [TOK, D], F32)
    qT = dsc("qT", [16 * 128, TOK], BF16)
    kx_loc = dsc("kx_loc", [512, TOK], BF16)
    vx_loc = dsc("vx_loc", [512, NT * 128], BF16)
    if NP > 1:
        kx_all = dsc("kx_all", [NP * 512, TOK], BF16)
        vx_all = dsc("vx_all", [NP * 512, NT * 128], BF16)
        send_all = dsc("send_all", [NP * 128, 4096], F32)
    agT = dsc("agT", [2048, TOK], BF16)
    rqT = dsc("rqT", [1024, TOK], BF16)
    rkT = dsc("rkT", [1024, TOK], BF16)
    rv_d = dsc("rv_d", [TOK, 2048], BF16)
    rg_d = dsc("rg_d", [TOK, 2048], BF16)
    U_d = dsc("U_d", [NT * 128, 4096], F32)
    send_loc = dsc("send_loc", [128, 4096], F32)
    Sst = dsc("Sst", [NT * 128, 4096], BF16)
    if NP == 1:
        kx_all, vx_all, send_all = kx_loc, vx_loc, send_loc

    with ExitStack() as es:
        k = K(nc, es)
        cc_sem = es.enter_context(nc.semaphore("cc"))
        cc_cnt = [0]
        ps = es.enter_context(nc.psum_tensor("ps", [128, 8, 512], F32))
        tpb = [T("bank%d" % i, ex=True) for i in range(8)]

        sbn = [0]

        def sb(st, name, shape, dt=F32):
            sbn[0] += 1
            return st.enter_context(nc.sbuf_tensor("s%d_%s" % (sbn[0], name), shape, dt))

        cst = sb(es, "cst", [128, 1024])
        ident16 = sb(es, "ident16", [128, 128], BF16)
        ones16 = sb(es, "ones16", [128, 128], BF16)
        gn = sb(es, "gn", [128, 2048])
        nwT = sb(es, "nwT", [128, 32])
        qkg = sb(es, "qkg", [128, 256])
        dec = sb(es, "dec", [128, 16])
        lg = sb(es, "lg", [128, 16])
        ptmp = sb(es, "ptmp", [128, 16])
        zeta = sb(es, "zeta", [128, 16])
        gC = sb(es, "gC", [128, 16])
        gCj = sb(es, "gCj", [128, NT, 8])
        etmp = sb(es, "etmp", [128, 128])
        nw32 = sb(es, "nw32", [32, 128])
        tnw32 = T()
        tcst, tid, tones, tgn, tDT, txi, tnw, tqkg, tdec, tlg, tptmp, tzeta, tgC, tgCj, tetmp = [T() for _ in range(15)]

        IDF = cst[:, 0:128]
        QP1 = cst[:, 128:256]
        CMQ = cst[:, 256:384]
        DPOS = cst[:, 384:512]
        DNEG = cst[:, 512:640]
        CM1P = cst[:, 640:641]
        PIDX = cst[:, 641:642]
        SELF = cst[:, 642:643]
        SELB = cst[:, 643:644]
        EPSC = cst[:, 644:645]

        k.dma("sp", cst[:], cst_d, writes=[tcst])
        k.op("dve", lambda e: e.tensor_copy(out=ident16[:], in_=IDF), reads=[tcst], writes=[tid])
        k.op("dve", lambda e: e.memset(ones16[:], 1.0), writes=[tones])

        twi = [[T("wi%d_%d" % (l_, b_)) for b_ in range(22)] for l_ in range(L)]
        two_ = [[T("wo%d_%d" % (l_, b_)) for b_ in range(8)] for l_ in range(L)]
        cast_q = []
        for l_ in range(L):
            wv_i = w_in.ap()[l_].rearrange("(c p) n -> p c n", p=128)
            for (_kd, b_) in BLOCK_ORDER:
                dst = wibs[l_][b_ * 128:(b_ + 1) * 128, :].rearrange("p (c n) -> p c n", n=512)
                for c8 in range(4):
                    cast_q.append((dst[:, c8 * 8:(c8 + 1) * 8, :], wv_i[:, c8 * 8:(c8 + 1) * 8, b_ * 512:(b_ + 1) * 512], twi[l_][b_]))
            wv_o = w_out.ap()[l_].rearrange("(c p) n -> p c n", p=128)
            for b_ in range(8):
                dst = wobs[l_][b_ * 128:(b_ + 1) * 128, :].rearrange("p (c n) -> p c n", n=512)
                for c8 in range(4):
                    cast_q.append((dst[:, c8 * 8:(c8 + 1) * 8, :], wv_o[:, c8 * 8:(c8 + 1) * 8, b_ * 512:(b_ + 1) * 512], two_[l_][b_]))
        cast_pos = [0]
        cast_last = {}
        for i_, (_d, _s, t_) in enumerate(cast_q):
            cast_last[id(t_)] = i_

        def need_cast(t_):
            while cast_pos[0] <= cast_last[id(t_)]:
                pump(1)

        def pump(n):
            for _ in range(n):
                if cast_pos[0] < len(cast_q):
                    d_, s_, t_ = cast_q[cast_pos[0]]
                    cast_pos[0] += 1
                    k.dma("pool", d_, s_, writes=[t_])

        pump(88)

        def rstd_chain(v_ap, tv, scale):
            k.op("dve", lambda e: e.tensor_scalar(out=v_ap, in0=v_ap, scalar1=scale, scalar2=EPS,
                                                   op0=ALU.mult, op1=ALU.add), reads=[tv], writes=[tv])
            k.op("act", lambda e: e.activation(out=v_ap, in_=v_ap, func=AF.Ln), reads=[tv], writes=[tv])
            k.op("act", lambda e: e.activation(out=v_ap, in_=v_ap, func=AF.Exp, scale=-0.5), reads=[tv], writes=[tv])

        for l in range(L):
            xsrc = x_in if (l == 0 and x_from_input) else xres
            k.dma("sp", gn[:], bass.AP(ret_norm, l * 2048, [[0, 128], [1, 2048]]), writes=[tgn])
            k.dma("sp", qkg[:, 0:128], bass.AP(q_norm, l * 128, [[0, 128], [1, 128]]), writes=[tqkg])
            k.dma("sp", qkg[:, 128:256], bass.AP(k_norm, l * 128, [[0, 128], [1, 128]]), writes=[tqkg])
            k.op("dve", lambda e: e.tensor_scalar(out=qkg[:, 0:128], in0=qkg[:, 0:128], scalar1=float(128 ** -0.5),
                                                   scalar2=0.0, op0=ALU.mult, op1=ALU.add), reads=[tqkg], writes=[tqkg])
            k.dma("sp", dec[:], bass.AP(rdec, l * 16, [[0, 128], [1, 16]]), writes=[tdec])
            k.dma("sp", nw32[0:32, :], bass.AP(norm_w, l * D, [[128, 32], [1, 128]]), writes=[tnw32])
            k.op("pe", lambda e: e.matmul(ps[:, 4, 0:32], nw32[0:32, :], cst[0:32, 0:32], start=True, stop=True),
                 reads=[tnw32, tcst], writes=[tpb[4]])
            k.op("dve", lambda e: e.tensor_copy(out=nwT[:], in_=ps[:, 4, 0:32]), reads=[tpb[4]], writes=[tnw])
            k.op("act", lambda e: e.activation(out=lg[:], in_=dec[:], func=AF.Exp, scale=-1.0), reads=[tdec], writes=[tlg])
            k.op("dve", lambda e: e.tensor_scalar(out=ptmp[:], in0=lg[:], scalar1=1.0 / 7, scalar2=-1.0 / 6,
                                                   op0=ALU.mult, op1=ALU.add), reads=[tlg], writes=[tptmp])
            for cc in (1.0 / 5, -1.0 / 4, 1.0 / 3, -1.0 / 2, 1.0):
                k.op("dve", lambda e: e.tensor_tensor(out=ptmp[:], in0=ptmp[:], in1=lg[:], op=ALU.mult),
                     reads=[tptmp, tlg], writes=[tptmp])
                k.op("dve", lambda e: e.tensor_scalar(out=ptmp[:], in0=ptmp[:], scalar1=1.0, scalar2=cc,
                                                       op0=ALU.mult, op1=ALU.add), reads=[tptmp], writes=[tptmp])
            k.op("dve", lambda e: e.tensor_tensor(out=ptmp[:], in0=ptmp[:], in1=lg[:], op=ALU.mult),
                 reads=[tptmp, tlg], writes=[tptmp])
            k.op("dve", lambda e: e.tensor_scalar(out=lg[:], in0=ptmp[:], scalar1=-1.0, scalar2=0.0,
                                                   op0=ALU.mult, op1=ALU.add), reads=[tptmp], writes=[tlg])
            k.op("act", lambda e: e.activation(out=zeta[:, 0:8], in_=lg[:, 0:8], func=AF.Exp, scale=CM1P),
                 reads=[tcst, tlg], writes=[tzeta])
            k.op("act", lambda e: e.activation(out=zeta[:, 8:16], in_=lg[:, 8:16], func=AF.Exp, scale=PIDX),
                 reads=[tcst, tlg], writes=[tzeta])
            k.op("act", lambda e: e.activation(out=gC[:], in_=lg[:], func=AF.Exp, scale=128.0), reads=[tlg], writes=[tgC])
            for j in range(NT):
                k.op("act", lambda e: e.activation(out=gCj[:, j, :], in_=lg[:, 8:16], func=AF.Exp, scale=128.0 * j),
                     reads=[tlg], writes=[tgCj])

            if STOP == "tables":
                k.barrier()
                break
            with ExitStack() as sa:
                hT = sb(sa, "hT", [128, 32, 512], BF16)
                wt = [sb(sa, "wt%d" % i, [128, 16, 512], BF16) for i in range(2)]
                xt2 = [sb(sa, "xt%d" % i, [128, D]) for i in range(2)]
                h162 = [sb(sa, "h16%d" % i, [128, D], BF16) for i in range(2)]
                txt2 = [T(), T()]
                th162 = [T(), T()]
                ss2 = [sb(sa, "ss%d" % i, [128, 1]) for i in range(2)]
                tss2 = [T(), T()]
                rvs = sb(sa, "rvs", [128, 4, 2048], BF16)
                ust = [sb(sa, "ust%d" % i, [128, 4, 256]) for i in range(2)]
                send = sb(sa, "send", [128, 2, 8, 256])
                ropet = sb(sa, "ropet", [128, 4, 256])
                ssq4 = [sb(sa, "ssq%d" % i, [128, 4]) for i in range(4)]
                xn4 = [sb(sa, "xn4_%d" % i, [128, 512]) for i in range(4)]
                sq = sb(sa, "sq", [128, 512])
                tmpA = sb(sa, "tmpA", [128, 512])
                tmpT = sb(sa, "tmpT", [128, 512])
                xrb = [[sb(sa, "xr%d_%d" % (p_, t_), [128, 512], BF16) for t_ in range(4)] for p_ in range(2)]
                txrb = [[T() for t_ in range(4)] for p_ in range(2)]
                rkzb = [sb(sa, "rkz%d" % t_, [128, 2, 512], BF16) for t_ in range(4)]
                trkzb = [T() for t_ in range(4)]
                deferred = []
                blk_i = [0]

                chain2s = []

                def run_chain2():
                    for f_ in chain2s:
                        f_()
                    del chain2s[:]

                def flush():
                    for f_ in deferred:
                        f_()
                    del deferred[:]
                stg = [sb(sa, "stg%d" % i, [128, 512], BF16) for i in range(4)]
                thT, tsend, tropet, tsq, ttA, ttT = [T() for _ in range(6)]
                trvs4 = [[T() for _ in range(4)] for _ in range(4)]
                tssq4 = [T() for _ in range(4)]
                txn4 = [T() for _ in range(4)]
                twt = [T(), T()]
                tust = [T(), T()]
                tstg = [T() for _ in range(4)]
                stg_i = [0]
                evac_i = [0]
                acc_i = [0]

                def next_stg():
                    i = stg_i[0] % 4
                    stg_i[0] += 1
                    return stg[i], tstg[i]

                def evac_copy(out_ap, in_ap, reads, writes, scale=None):
                    evac_i[0] += 1
                    if False:
                        k.op("dve", lambda e: e.tensor_copy(out=out_ap, in_=in_ap), reads=reads, writes=writes)
                    else:
                        if scale is None:
                            k.op("act", lambda e: e.activation(out=out_ap, in_=in_ap, func=AF.Copy), reads=reads, writes=writes)
                        else:
                            k.op("act", lambda e: e.activation(out=out_ap, in_=in_ap, func=AF.Copy, scale=scale),
                                 reads=reads, writes=writes)

                k.op("dve", lambda e: e.memset(send[:], 0.0), writes=[tsend])

                def rope(src, tsrc, dst, tdst, tt):
                    C_ = fap(ropet[:, tt, 0:128], [[0, 4], [1, 128]])
                    k.op("dve", lambda e: e.tensor_tensor(out=fap(tmpA[:], [[128, 4], [1, 128]]),
                                                           in0=fap(src, [[128, 4], [1, 128]]), in1=C_, op=ALU.mult),
                         reads=[tsrc, tropet], writes=[ttA])
                    pat = [[128, 4], [64, 2], [1, 32]]
                    spat = [[0, 4], [64, 2], [1, 32]]
                    k.op("dve", lambda e: e.tensor_tensor(out=fap(tmpT[:, 0:32], pat), in0=fap(src[:, 32:64], pat),
                                                           in1=fap(ropet[:, tt, 128:160], spat), op=ALU.mult),
                         reads=[tsrc, tropet], writes=[ttT])
                    k.op("dve", lambda e: e.tensor_tensor(out=fap(tmpT[:, 32:64], pat), in0=fap(src[:, 0:32], pat),
                                                           in1=fap(ropet[:, tt, 160:192], spat), op=ALU.mult),
                         reads=[tsrc, tropet], writes=[ttT])
                    k.op("dve", lambda e: e.tensor_tensor(out=dst, in0=tmpA[:], in1=tmpT[:], op=ALU.add),
                         reads=[ttA, ttT], writes=[tdst])

                def transposes_out(src16, tsrc, dram_rows, j):
                    for c4 in range(4):
                        k.op("pe", lambda e: e.matmul(ps[:, 4, c4 * 128:(c4 + 1) * 128], src16[:, c4 * 128:(c4 + 1) * 128],
                                                       ident16[:], start=True, stop=True),
                             reads=[tsrc, tid], writes=[tpb[4]], inc=(c4 == 3))
                    s_, ts_ = next_stg()
                    evac_copy(s_[:], ps[:, 4, :], [tpb[4]], [ts_])
                    k.dma("sp", dram_rows[:, j * 128:(j + 1) * 128].rearrange("(h d) t -> d h t", d=128),
                          fap(s_[:], [[128, 4], [1, 128]]), reads=[ts_])

                witems = [(b_, half_) for _tb in range(NTB) for (_kd, b_) in BLOCK_ORDER for half_ in range(2)]

                def emit_wload(n_):
                    if n_ < len(witems):
                        b_, half_ = witems[n_]
                        pump(1)
                        need_cast(twi[l][b_])
                        k.dma("sp", wt[n_ % 2][:], wibs[l][b_ * 128:(b_ + 1) * 128, half_ * 8192:(half_ + 1) * 8192]
                              .rearrange("p (c n) -> p c n", n=512), reads=[twi[l][b_]], writes=[twt[n_ % 2]])

                emit_wload(0)
                emit_wload(1)
                for tb in range(NTB):
                    if tb == 0:
                        k.dma("sp", ropet[:], rope_d[tb * 512:(tb + 1) * 512, :].rearrange("(j p) c -> p j c", p=128),
                              writes=[tropet])
                    for tt in range(4):
                        j = tb * 4 + tt
                        xt, txt, h16, th16, ss, tss = xt2[j % 2], txt2[j % 2], h162[j % 2], th162[j % 2], ss2[j % 2], tss2[j % 2]
                        if tb == 0 or tt >= 2:
                            k.dma("sp", xt[:], xsrc[j * 128:(j + 1) * 128, :], writes=[txt])
                        k.op("act", lambda e: e.activation(out=h16[:], in_=xt[:], func=AF.Square, accum_out=ss[:, 0:1]),
                             reads=[txt], writes=[th16, tss])
                        k.op("act", lambda e: e.activation(out=ss[:, 0:1], in_=ss[:, 0:1], func=AF.Ln, scale=1.0 / D, bias=EPSC),
                             reads=[tss, tcst], writes=[tss])
                        k.op("act", lambda e: e.activation(out=ss[:, 0:1], in_=ss[:, 0:1], func=AF.Exp, scale=-0.5), reads=[tss], writes=[tss])
                        k.op("dve", lambda e: e.tensor_scalar(out=h16[:], in0=xt[:], scalar1=ss[:, 0:1], scalar2=0.0,
                                                               op0=ALU.mult, op1=ALU.add), reads=[txt, tss], writes=[th16])
                        for g in range(8):
                            bank = 4 + (g % 2)
                            for c4 in range(4):
                                c = g * 4 + c4
                                k.op("pe", lambda e: e.matmul(ps[:, bank, c4 * 128:(c4 + 1) * 128], h16[:, c * 128:(c + 1) * 128],
                                                               ident16[:], start=True, stop=True),
                                     reads=[th16, tid], writes=[tpb[bank]], inc=(c4 == 3))
                            k.op("dve", lambda e: e.tensor_tensor(out=hT[:, g * 4:(g + 1) * 4, tt * 128:(tt + 1) * 128],
                                                                   in0=fap(ps[:, bank, :], [[128, 4], [1, 128]]),
                                                                   in1=fap(nwT[:, g * 4:(g + 1) * 4], [[1, 4], [0, 128]]),
                                                                   op=ALU.mult),
                                 reads=[tpb[bank], tnw], writes=[thT])
                    for bpos, (kind, b) in enumerate(BLOCK_ORDER):
                        if bpos == 18 and tb + 1 < NTB:
                            k.dma("sp", ropet[:], rope_d[(tb + 1) * 512:(tb + 2) * 512, :].rearrange("(j p) c -> p j c", p=128),
                                  writes=[tropet])
                            for t2 in range(2):
                                jn = (tb + 1) * 4 + t2
                                k.dma("sp", xt2[jn % 2][:], xsrc[jn * 128:(jn + 1) * 128, :], writes=[txt2[jn % 2]])
                        wrow = b * 128
                        wib = wibs[l]
                        if kind != "ag":
                            banks = [0, 1, 2, 3]
                            for half in range(2):
                                wi = acc_i[0] % 2
                                for tt in range(4):
                                    for c in range(16):
                                        cg = half * 16 + c
                                        k.op("pe", lambda e: e.matmul(ps[:, banks[tt], :], hT[:, cg, tt * 128:(tt + 1) * 128],
                                                                       wt[wi][:, c, :], start=(cg == 0), stop=(cg == 31)),
                                             reads=[thT, twt[wi]], writes=[tpb[banks[tt]]], inc=(c == 15))
                                emit_wload(acc_i[0] + 2)
                                acc_i[0] += 1
                            flush()
                            par = blk_i[0] % 2
                            blk_i[0] += 1
                            for tt in range(4):
                                j = tb * 4 + tt
                                pb = ps[:, banks[tt], :]
                                tb_ = tpb[banks[tt]]
                                xr = xrb[par][tt]
                                txr = txrb[par][tt]
                                rkz = rkzb[tt]
                                trkz = trkzb[tt]
                                xn = xn4[tt]
                                txn = txn4[tt]
                                if kind in ("aq", "ak"):
                                    goff = 0 if kind == "aq" else 128
                                    for hh in range(4):
                                        k.op("act", lambda e: e.activation(out=sq[:, hh * 128:(hh + 1) * 128], in_=pb[:, hh * 128:(hh + 1) * 128],
                                                                            func=AF.Square, accum_out=ssq4[tt][:, hh:hh + 1]),
                                             reads=[tb_], writes=[tsq, tssq4[tt]])
                                    k.op("act", lambda e: e.activation(out=ssq4[tt][:], in_=ssq4[tt][:], func=AF.Ln, scale=1.0 / 128, bias=EPSC),
                                         reads=[tssq4[tt], tcst], writes=[tssq4[tt]])
                                    k.op("act", lambda e: e.activation(out=ssq4[tt][:], in_=ssq4[tt][:], func=AF.Exp, scale=-0.5),
                                         reads=[tssq4[tt]], writes=[tssq4[tt]])
                                    k.op("dve", lambda e: e.tensor_tensor(out=fap(xn[:], [[128, 4], [1, 128]]),
                                                                           in0=fap(pb, [[128, 4], [1, 128]]),
                                                                           in1=fap(ssq4[tt][:], [[1, 4], [0, 128]]), op=ALU.mult),
                                         reads=[tb_, tssq4[tt]], writes=[txn])
                                    rows_ = qT[b * 512:(b + 1) * 512, :] if kind == "aq" else kx_loc

                                    def chain2(xn=xn, txn=txn, xr=xr, txr=txr, tt=tt, goff=goff, rows_=rows_, j=j):
                                        k.op("dve", lambda e: e.tensor_tensor(out=fap(xn[:], [[128, 4], [1, 128]]),
                                                                               in0=fap(xn[:], [[128, 4], [1, 128]]),
                                                                               in1=fap(qkg[:, goff:goff + 128], [[0, 4], [1, 128]]),
                                                                               op=ALU.mult),
                                             reads=[txn, tqkg], writes=[txn])
                                        rope(xn[:], txn, xr[:], txr, tt)
                                        deferred.append(lambda xr_=xr, t_=txr, r_=rows_, j_=j: transposes_out(xr_, t_, r_, j_))
                                    chain2s.append(chain2)
                                elif kind == "av":
                                    s_, ts_ = next_stg()
                                    evac_copy(s_[:], pb, [tb_], [ts_])
                                    k.dma("sp", vx_loc[:, j * 128:(j + 1) * 128].rearrange("(g p) d -> p g d", p=128),
                                          fap(s_[:], [[128, 4], [1, 128]]), reads=[ts_])
                                elif kind == "rq":
                                    evac_copy(xn[:], pb, [tb_], [txn])
                                    rope(xn[:], txn, xr[:], txr, tt)
                                    hb = b - 10
                                    deferred.append(lambda xr_=xr, t_=txr, r_=rqT[hb * 512:(hb + 1) * 512, :], j_=j: transposes_out(xr_, t_, r_, j_))
                                elif kind == "rk":
                                    evac_copy(xn[:], pb, [tb_], [txn], scale=float(128 ** -0.5))
                                    rope(xn[:], txn, xr[:], txr, tt)
                                    hb = b - 12
                                    deferred.append(lambda xr_=xr, t_=txr, r_=rkT[hb * 512:(hb + 1) * 512, :], j_=j: transposes_out(xr_, t_, r_, j_))
                                    for dr in range(2):
                                        zoff = dr * 8 + hb * 4
                                        k.op("dve", lambda e: e.tensor_tensor(out=fap(rkz[:, dr, :], [[128, 4], [1, 128]]),
                                                                               in0=fap(xr[:], [[128, 4], [1, 128]]),
                                                                               in1=fap(zeta[:, zoff:zoff + 4], [[1, 4], [0, 128]]),
                                                                               op=ALU.mult),
                                             reads=[txr, tzeta], writes=[trkz])

                                    def u_part(rkz=rkz, trkz=trkz, tt=tt, j=j, hb=hb):
                                        for dr in range(2):
                                            bb = 6 if dr == 0 else 4
                                            for hh in range(4):
                                                h = hb * 4 + hh
                                                bank = bb + hh // 2
                                                k.op("pe", lambda e: e.matmul(ps[:, bank, (hh % 2) * 256:(hh % 2 + 1) * 256],
                                                                               rkz[:, dr, hh * 128:(hh + 1) * 128],
                                                                               rvs[:, tt, h * 256:(h + 1) * 256], start=True, stop=True),
                                                     reads=[trkz, trvs4[tt][h // 2]], writes=[tpb[bank]], inc=(hh % 2 == 1))
                                        for dr in range(2):
                                            bb = 6 if dr == 0 else 4
                                            u_ = ust[dr]
                                            tu_ = tust[dr]
                                            k.op("act", lambda e: e.activation(out=fap(u_[:, 0:2, :], [[1, 512]]), in_=ps[:, bb, :], func=AF.Copy),
                                                 reads=[tpb[bb]], writes=[tu_])
                                            k.op("dve", lambda e: e.tensor_copy(out=fap(u_[:, 2:4, :], [[1, 512]]), in_=ps[:, bb + 1, :]),
                                                 reads=[tpb[bb + 1]], writes=[tu_])
                                            k.dma("sp", U_d[j * 128:(j + 1) * 128, dr * 2048 + hb * 1024: dr * 2048 + (hb + 1) * 1024],
                                                  fap(u_[:], [[1, 1024]]), reads=[tu_])
                                            for hh in range(4):
                                                h = hb * 4 + hh
                                                if dr == 0:
                                                    k.op("dve", lambda e: e.scalar_tensor_tensor(out=send[:, 0, h, :], in0=send[:, 0, h, :],
                                                                                                  scalar=gC[:, h:h + 1], in1=u_[:, hh, :],
                                                                                                  op0=ALU.mult, op1=ALU.add),
                                                         reads=[tu_, tgC], writes=[tsend])
                                                else:
                                                    k.op("dve", lambda e: e.scalar_tensor_tensor(out=send[:, 1, h, :], in0=u_[:, hh, :],
                                                                                                  scalar=gCj[:, j, h:h + 1], in1=send[:, 1, h, :],
                                                                                                  op0=ALU.mult, op1=ALU.add),
                                                         reads=[tu_, tgCj], writes=[tsend])
                                    deferred.append(u_part)
                                elif kind == "rv":
                                    hb = b - 14
                                    evac_copy(rvs[:, tt, hb * 512:(hb + 1) * 512], pb, [tb_], [trvs4[tt][hb]])
                                    k.dma("sp", rv_d[j * 128:(j + 1) * 128, hb * 512:(hb + 1) * 512],
                                          rvs[:, tt, hb * 512:(hb + 1) * 512], reads=[trvs4[tt][hb]])
                                elif kind == "rg":
                                    hb = b - 18
                                    k.op("act", lambda e: e.activation(out=sq[:], in_=pb, func=AF.Silu), reads=[tb_], writes=[tsq])
                                    s_, ts_ = next_stg()
                                    k.op("dve", lambda e: e.tensor_tensor(out=s_[:], in0=sq[:], in1=gn[:, hb * 512:(hb + 1) * 512],
                                                                           op=ALU.mult), reads=[tsq, tgn], writes=[ts_])
                                    k.dma("sp", rg_d[j * 128:(j + 1) * 128, hb * 512:(hb + 1) * 512], s_[:], reads=[ts_])
                            run_chain2()
                        else:
                            hb = b - 6
                            for half in range(2):
                                wi = acc_i[0] % 2
                                for cb in range(4):
                                    for c in range(16):
                                        cg = half * 16 + c
                                        k.op("pe", lambda e: e.matmul(ps[:, cb, :], wt[wi][:, c, cb * 128:(cb + 1) * 128],
                                                                       hT[:, cg, :], start=(cg == 0), stop=(cg == 31)),
                                             reads=[thT, twt[wi]], writes=[tpb[cb]], inc=(c == 15))
                                emit_wload(acc_i[0] + 2)
                                acc_i[0] += 1
                            flush()
                            for cb in range(4):
                                s_, ts_ = next_stg()
                                k.op("act", lambda e: e.activation(out=s_[:], in_=ps[:, cb, :], func=AF.Silu),
                                     reads=[tpb[cb]], writes=[ts_])
                                r0 = hb * 512 + cb * 128
                                k.dma("sp", agT[r0:r0 + 128, tb * 512:(tb + 1) * 512], s_[:], reads=[ts_])
                    flush()
                k.dma("sp", send_loc, fap(send[:], [[1, 4096]]), reads=[tsend])
                k.barrier()
            if STOP == "A":
                break
            if NP > 1:
                groups = [[2 * i, 2 * i + 1] for i in range(4)] if NP == 2 else None
                for (a_, b_) in ((kx_loc, kx_all), (vx_loc, vx_all), (send_loc, send_all)):
                    nc.gpsimd.collective_compute("AllGather", ALU.bypass, replica_groups=groups,
                                                 ins=[a_], outs=[b_]).then_inc(cc_sem)
                    cc_cnt[0] += 1
                for eng in k.E.values():
                    eng.wait_ge(cc_sem, cc_cnt[0])

            tSst = T("Sst")
            with ExitStack() as spl:
                Sacc = [sb(spl, "Sacc%d" % i, [128, 8, 256]) for i in range(2)]
                ut = [[sb(spl, "ut%d_%d" % (d_, i), [128, 8, 256]) for i in range(2)] for d_ in range(2)]
                s16 = [[sb(spl, "s16%d_%d" % (d_, i), [128, 8, 256], BF16) for i in range(2)] for d_ in range(2)]
                tSacc = [T(), T()]
                tut = [[T(), T()], [T(), T()]]
                ts16 = [[T(), T()], [T(), T()]]
                for dr in range(2):
                    sel = SELF if dr == 0 else SELB
                    rk_ = 0 if dr == 0 else NP - 1
                    k.dma("sp", fap(ut[dr][1][:], [[1, 2048]]), send_all[rk_ * 128:(rk_ + 1) * 128, dr * 2048:(dr + 1) * 2048],
                          writes=[tut[dr][1]])
                    k.op("dve", lambda e: e.tensor_scalar(out=fap(Sacc[dr][:], [[1, 2048]]), in0=fap(ut[dr][1][:], [[1, 2048]]), scalar1=sel,
                                                           scalar2=0.0, op0=ALU.mult, op1=ALU.add),
                         reads=[tut[dr][1], tcst], writes=[tSacc[dr]])
                for step in range(NT):
                    for dr in range(2):
                        j = step if dr == 0 else NT - 1 - step
                        bi_ = step % 2
                        u_, tu_ = ut[dr][bi_], tut[dr][bi_]
                        s_, ts_ = s16[dr][bi_], ts16[dr][bi_]
                        k.dma("sp", fap(u_[:], [[1, 2048]]), U_d[j * 128:(j + 1) * 128, dr * 2048:(dr + 1) * 2048], writes=[tu_])
                        k.op("act", lambda e: e.activation(out=fap(s_[:], [[1, 2048]]), in_=fap(Sacc[dr][:], [[1, 2048]]), func=AF.Copy),
                             reads=[tSacc[dr]], writes=[ts_])
                        k.dma("sp", Sst[j * 128:(j + 1) * 128, dr * 2048:(dr + 1) * 2048], fap(s_[:], [[1, 2048]]),
                              reads=[ts_], writes=[tSst])
                        for h in range(8):
                            k.op("dve", lambda e: e.scalar_tensor_tensor(out=Sacc[dr][:, h, :], in0=Sacc[dr][:, h, :],
                                                                          scalar=gC[:, dr * 8 + h:dr * 8 + h + 1], in1=u_[:, h, :],
                                                                          op0=ALU.mult, op1=ALU.add),
                                 reads=[tu_, tgC], writes=[tSacc[dr]])
                k.barrier()
            if STOP == "pro":
                break
            with ExitStack() as sbk:
                DT = sb(sbk, "DT", [128, 8, 128])
                xi = sb(sbk, "xi", [128, 2, 8, 128])
                tDT, txi = T(), T()
                for h in range(8):
                    k.op("act", lambda e: e.activation(out=xi[:, 0, h, :], in_=QP1, func=AF.Exp, scale=lg[:, h:h + 1]),
                         reads=[tcst, tlg], writes=[txi])
                    k.op("act", lambda e: e.activation(out=xi[:, 1, h, :], in_=CMQ, func=AF.Exp, scale=lg[:, 8 + h:9 + h]),
                         reads=[tcst, tlg], writes=[txi])
                    k.op("act", lambda e: e.activation(out=DT[:, h, :], in_=DPOS, func=AF.Exp, scale=lg[:, h:h + 1]),
                         reads=[tcst, tlg], writes=[tDT])
                    k.op("act", lambda e: e.activation(out=etmp[:], in_=DNEG, func=AF.Exp, scale=lg[:, 8 + h:9 + h]),
                         reads=[tcst, tlg], writes=[tetmp])
                    k.op("dve", lambda e: e.tensor_tensor(out=DT[:, h, :], in0=DT[:, h, :], in1=etmp[:], op=ALU.mult),
                         reads=[tetmp], writes=[tDT])
                mixT = sb(sbk, "mixT", [128, 32, 512], BF16)
                wo = [sb(sbk, "wo%d" % i, [128, 16, 512], BF16) for i in range(2)]
                v_sb = [sb(sbk, "v_sb%d" % i, [128, NKT * 128], BF16) for i in range(2)]
                kt_sb = [sb(sbk, "kt_sb%d" % i, [128, SF], BF16) for i in range(2)]
                q_sb = [sb(sbk, "q_sb%d" % i, [128, 4, 512], BF16) for i in range(2)]
                ag_sb = sb(sbk, "ag_sb", [128, 4, 512], BF16)
                pt = [sb(sbk, "pt%d" % i, [128, 512], BF16) for i in range(8)]
                sacc = [sb(sbk, "sacc%d" % i, [128, 512]) for i in range(2)]
                sacc16 = [sb(sbk, "sacc16%d" % i, [128, 512], BF16) for i in range(2)]
                tsacc = [T(), T()]
                tsacc16 = [T(), T()]
                rc = [sb(sbk, "rc%d" % i, [128, 512]) for i in range(2)]
                o32 = [sb(sbk, "o32%d" % i, [128, 512]) for i in range(2)]
                rq_sb = [sb(sbk, "rq_sb%d" % i, [128, 8, 128], BF16) for i in range(2)]
                rk_sb = [sb(sbk, "rk_sb%d" % i, [128, 8, 128], BF16) for i in range(2)]
                ret_i = [0]
                rv_sb = sb(sbk, "rv_sb", [128, 2048], BF16)
                rg_sb = sb(sbk, "rg_sb", [128, 2048], BF16)
                S_sb_t = sb(sbk, "S_sb", [128, 4096], BF16)
                rv_sb_t = rv_sb
                rg_sb_t = rg_sb
                S_sb, rv_sb, rg_sb = S_sb_t[:], rv_sb_t[:], rg_sb_t[:]
                ptr = [sb(sbk, "ptr%d" % i, [128, 128], BF16) for i in range(8)]
                qx = [sb(sbk, "qx%d" % i, [128, 2, 128], BF16) for i in range(8)]
                r16 = [sb(sbk, "r16%d" % i, [128, 2048], BF16) for i in range(2)]
                junk = sb(sbk, "junk", [128, 256], BF16)
                ssq8 = sb(sbk, "ssq8", [128, 8])
                xin = [sb(sbk, "xin%d" % i, [128, 512]) for i in range(2)]
                xo = [sb(sbk, "xo%d" % i, [128, 512]) for i in range(2)]
                tmix, tag, trv, trg, tS, tjunk, tssq8 = [T() for _ in range(7)]
                tq = [T(), T()]
                trq = [T(), T()]
                trk = [T(), T()]
                tr16 = [T(), T()]
                trc = [T(), T()]
                to32 = [T(), T()]
                two = [T(), T()]
                tv = [T(), T()]
                tkt = [T(), T()]
                if SF == 4096:
                    S_alt, tS_alt = kt_sb[0][:], tkt[0]
                    rv_alt, trv_alt = v_sb[0][:, 0:2048], tv[0]
                    rg_alt, trg_alt = v_sb[0][:, 2048:4096], tv[0]
                else:
                    S_alt, tS_alt = sb(sbk, "S_alt", [128, 4096], BF16)[:], T()
                    rv_alt, trv_alt = sb(sbk, "rv_alt", [128, 2048], BF16)[:], T()
                    rg_alt, trg_alt = sb(sbk, "rg_alt", [128, 2048], BF16)[:], T()
                tpt = [T() for _ in range(8)]
                tptr = [T() for _ in range(8)]
                tqx = [T() for _ in range(8)]
                txin = [T(), T()]
                txo = [T(), T()]

                att_i = [0]
                kv_i = [0]
                wo_i = [0]
                woitems = [(cb_, half_) for _qb in range(NTB) for cb_ in range(8) for half_ in range(2)]

                def emit_woload(n_):
                    if n_ < len(woitems):
                        cb_, half_ = woitems[n_]
                        pump(1)
                        need_cast(two_[l][cb_])
                        k.dma("sp", wo[n_ % 2][:], wobs[l][cb_ * 128:(cb_ + 1) * 128, half_ * 8192:(half_ + 1) * 8192]
                              .rearrange("p (c n) -> p c n", n=512), reads=[two_[l][cb_]], writes=[two[n_ % 2]])

                emit_woload(0)
                emit_woload(1)
                def att_kv_loads(qb_, g_):
                    kvi_ = g_ % 2
                    for r in range(NP):
                        k.dma("sp", kt_sb[kvi_][:, r * TOK:(r + 1) * TOK], kx_all[r * 512 + g_ * 128: r * 512 + (g_ + 1) * 128, :],
                              writes=[tkt[kvi_]])
                        k.dma("sp", v_sb[kvi_][:, r * NT * 128:(r + 1) * NT * 128],
                              vx_all[r * 512 + g_ * 128: r * 512 + (g_ + 1) * 128, :], writes=[tv[kvi_]])

                def att_q_load(qb_, g_):
                    k.dma("sp", q_sb[g_ % 2][:], qT[g_ * 512:(g_ + 1) * 512, qb_ * 512:(qb_ + 1) * 512].rearrange("(h d) t -> d h t", d=128),
                          writes=[tq[g_ % 2]])

                def att_ag_load(qb_, g_):
                    k.dma("sp", ag_sb[:], agT[g_ * 512:(g_ + 1) * 512, qb_ * 512:(qb_ + 1) * 512].rearrange("(h d) t -> d h t", d=128),
                          writes=[tag])

                for qb in range(NTB):
                    for g in range(4):
                        kvi = g % 2
                        q_c, tq_c = q_sb[g % 2], tq[g % 2]
                        if qb == 0 and g == 0:
                            att_kv_loads(0, 0)
                            att_q_load(0, 0)
                            att_ag_load(0, 0)
                        if g < 3:
                            att_kv_loads(qb, g + 1)
                            att_q_load(qb, g + 1)
                        elif qb + 1 < NTB:
                            att_q_load(qb + 1, 0)
                        for hp in range(2):
                            items = [(kt_, hh) for kt_ in range(NKT) for hh in range(2)]

                            def pv(item, pi):
                                kt_, hh = item
                                k.op("pe", lambda e: e.matmul(ps[:, 2 + hh, :], v_sb[kvi][:, kt_ * 128:(kt_ + 1) * 128], pt[pi][:],
                                                               start=(kt_ == 0), stop=(kt_ == NKT - 1)),
                                     reads=[tv[kvi], tpt[pi]], writes=[tpb[2 + hh]], inc=True)
                                if kt_ % 3 == 0:
                                    k.op("pe", lambda e: e.matmul(ps[:, 4 + hh, :], ones16[:], pt[pi][:],
                                                                   start=(kt_ == 0), stop=False),
                                         reads=[tones, tpt[pi]], writes=[tpb[4 + hh]], inc=True)
                                elif kt_ == 1:
                                    k.op("dve", lambda e: e.tensor_copy(out=sacc[hh][:], in_=pt[pi][:]),
                                         reads=[tpt[pi]], writes=[tsacc[hh]])
                                else:
                                    k.op("dve", lambda e: e.tensor_tensor(out=sacc[hh][:], in0=sacc[hh][:], in1=pt[pi][:], op=ALU.add),
                                         reads=[tpt[pi], tsacc[hh]], writes=[tsacc[hh]])

                            LA = 3
                            sbanks = [0, 1, 6, 7]
                            pend = []
                            for item in items:
                                kt_, hh = item
                                hq = hp * 2 + hh
                                ai = att_i[0]
                                att_i[0] += 1
                                sbank = sbanks[ai % 4]
                                pi = ai % 8
                                k.op("pe", lambda e: e.matmul(ps[:, sbank, :], kt_sb[kvi][:, kt_ * 128:(kt_ + 1) * 128],
                                                               q_c[:, hq, :], start=True, stop=True),
                                     reads=[tkt[kvi], tq_c], writes=[tpb[sbank]], inc=True)
                                k.op("act", lambda e: e.activation(out=pt[pi][:], in_=ps[:, sbank, :], func=AF.Exp),
                                     reads=[tpb[sbank]], writes=[tpt[pi]])
                                pend.append((item, pi))
                                if len(pend) > LA:
                                    pv(*pend.pop(0))
                            while pend:
                                pv(*pend.pop(0))
                            for hh in range(2):
                                k.op("dve", lambda e: e.tensor_copy(out=sacc16[hh][:], in_=sacc[hh][:]), reads=[tsacc[hh]], writes=[tsacc16[hh]])
                                k.op("pe", lambda e: e.matmul(ps[:, 4 + hh, :], ones16[:], sacc16[hh][:], start=False, stop=True),
                                     reads=[tones, tsacc16[hh]], writes=[tpb[4 + hh]], inc=True)
                            for hh in range(2):
                                hq = hp * 2 + hh
                                k.op("act", lambda e: e.activation(out=rc[hh][:], in_=ps[:, 4 + hh, :], func=AF.Copy),
                                     reads=[tpb[4 + hh]], writes=[trc[hh]])
                                k.op("act", lambda e: e.activation(out=o32[hh][:], in_=ps[:, 2 + hh, :], func=AF.Copy),
                                     reads=[tpb[2 + hh]], writes=[to32[hh]])
                            for hh in range(2):
                                hq = hp * 2 + hh
                                k.op("dve", lambda e: e.reciprocal(out=rc[hh][:], in_=rc[hh][:]), reads=[trc[hh]], writes=[trc[hh]])
                                k.op("dve", lambda e: e.tensor_tensor(out=o32[hh][:], in0=o32[hh][:], in1=rc[hh][:], op=ALU.mult),
                                     reads=[to32[hh], trc[hh]], writes=[to32[hh]])
                                k.op("dve", lambda e: e.tensor_tensor(out=mixT[:, g * 4 + hq, :], in0=o32[hh][:], in1=ag_sb[:, hq, :],
                                                                       op=ALU.mult), reads=[to32[hh], tag], writes=[tmix])
                        if g < 3:
                            att_ag_load(qb, g + 1)
                        elif qb + 1 < NTB:
                            att_ag_load(qb + 1, 0)
                    ret_def = []
                    for cj in range(4):
                        j = qb * 4 + cj
                        rb = ret_i[0] % 2
                        ret_i[0] += 1
                        rq_c, rk_c, r16_c = rq_sb[rb], rk_sb[rb], r16[rb]
                        trq_c, trk_c, tr16_c = trq[rb], trk[rb], tr16[rb]
                        if cj % 2 == 0:
                            S_c, tS_c, rv_c, trv_c, rg_c, trg_c = S_sb, tS, rv_sb, trv, rg_sb, trg
                        else:
                            S_c, tS_c = S_alt, tS_alt
                            rv_c, trv_c, rg_c, trg_c = rv_alt, trv_alt, rg_alt, trg_alt
                        k.dma("sp", rq_c[:], rqT[:, j * 128:(j + 1) * 128].rearrange("(h d) t -> d h t", d=128), writes=[trq_c])
                        k.dma("sp", rk_c[:], rkT[:, j * 128:(j + 1) * 128].rearrange("(h d) t -> d h t", d=128), writes=[trk_c])
                        k.dma("sp", rv_c, rv_d[j * 128:(j + 1) * 128, :], writes=[trv_c])
                        k.dma("sp", rg_c, rg_d[j * 128:(j + 1) * 128, :], writes=[trg_c])
                        k.dma("sp", S_c, Sst[j * 128:(j + 1) * 128, :], reads=[tSst], writes=[tS_c])
                        for h in range(8):
                            sbank = 4 + h // 4
                            k.op("pe", lambda e: e.matmul(ps[:, sbank, (h % 4) * 128:(h % 4 + 1) * 128], rk_c[:, h, :], rq_c[:, h, :],
                                                           start=True, stop=True), reads=[trk_c, trq_c], writes=[tpb[sbank]], inc=(h % 4 == 3))
                        for h in range(8):
                            sbank = 4 + h // 4
                            k.op("dve", lambda e: e.tensor_tensor(out=ptr[h][:], in0=ps[:, sbank, (h % 4) * 128:(h % 4 + 1) * 128],
                                                                   in1=DT[:, h, :], op=ALU.mult),
                                 reads=[tpb[sbank], tDT], writes=[tptr[h]])
                            k.op("dve", lambda e: e.tensor_tensor(out=qx[h][:], in0=fap(rq_c[:, h, :], [[0, 2], [1, 128]]),
                                                                    in1=fap(xi[:, 0, h, :], [[1024, 2], [1, 128]]), op=ALU.mult),
                                 reads=[trq_c, txi], writes=[tqx[h]])
                        for h in range(8):
                            bank = h // 2
                            oc = (h % 2) * 256
                            k.op("pe", lambda e: e.matmul(ps[:, bank, oc:oc + 256], ptr[h][:], rv_c[:, h * 256:(h + 1) * 256],
                                                           start=True, stop=False), reads=[tptr[h], trv_c], writes=[tpb[bank]], inc=False)
                            k.op("pe", lambda e: e.matmul(ps[:, bank, oc:oc + 256], qx[h][:, 0, :], S_c[:, h * 256:(h + 1) * 256],
                                                           start=False, stop=False), reads=[tqx[h], tS_c], writes=[tpb[bank]], inc=False)
                            k.op("pe", lambda e: e.matmul(ps[:, bank, oc:oc + 256], qx[h][:, 1, :], S_c[:, (8 + h) * 256:(9 + h) * 256],
                                                           start=False, stop=True), reads=[tqx[h], tS_c], writes=[tpb[bank]], inc=(h % 2 == 1))
                        for f_ in ret_def:
                            f_()
                        del ret_def[:]
                        for h in range(8):
                            bank = h // 2
                            oc = (h % 2) * 256
                            k.op("act", lambda e: e.activation(out=junk[:], in_=ps[:, bank, oc:oc + 256], func=AF.Square,
                                                                accum_out=ssq8[:, h:h + 1]),
                                 reads=[tpb[bank]], writes=[tjunk, tssq8])
                        rstd_chain(ssq8[:], tssq8, 1.0 / 256)
                        for h in range(8):
                            bank = h // 2
                            oc = (h % 2) * 256
                            k.op("dve", lambda e: e.scalar_tensor_tensor(out=r16_c[:, h * 256:(h + 1) * 256], in0=ps[:, bank, oc:oc + 256],
                                                                          scalar=ssq8[:, h:h + 1], in1=rg_c[:, h * 256:(h + 1) * 256],
                                                                          op0=ALU.mult, op1=ALU.mult),
                                 reads=[tpb[bank], tssq8, trg_c], writes=[tr16_c])

                        def tr_part(r16_c=r16_c, tr16_c=tr16_c, cj=cj):
                            for g4 in range(4):
                                tbk = 6 + g4 % 2
                                for c4 in range(4):
                                    c = g4 * 4 + c4
                                    k.op("pe", lambda e: e.matmul(ps[:, tbk, c4 * 128:(c4 + 1) * 128], r16_c[:, c * 128:(c + 1) * 128], ident16[:],
                                                                   start=True, stop=True), reads=[tr16_c, tid], writes=[tpb[tbk]], inc=(c4 == 3))
                                k.op("act", lambda e: e.activation(out=mixT[:, 16 + g4 * 4:16 + (g4 + 1) * 4, cj * 128:(cj + 1) * 128],
                                                                    in_=fap(ps[:, tbk, :], [[128, 4], [1, 128]]), func=AF.Copy),
                                     reads=[tpb[tbk]], writes=[tmix])
                        ret_def.append(tr_part)
                    for f_ in ret_def:
                        f_()
                    del ret_def[:]
                    if qb + 1 < NTB:
                        att_kv_loads(qb + 1, 0)
                    for cb in range(8):
                        wrow = cb * 128
                        wob = wobs[l]
                        banks = [4, 5, 6, 7] if cb % 2 == 0 else [0, 1, 2, 3]
                        for half in range(2):
                            wi = wo_i[0] % 2
                            for tt in range(4):
                                for c in range(16):
                                    cg = half * 16 + c
                                    k.op("pe", lambda e: e.matmul(ps[:, banks[tt], :], mixT[:, cg, tt * 128:(tt + 1) * 128],
                                                                   wo[wi][:, c, :], start=(cg == 0), stop=(cg == 31)),
                                         reads=[tmix, two[wi]], writes=[tpb[banks[tt]]], inc=(c == 15))
                            emit_woload(wo_i[0] + 2)
                            wo_i[0] += 1
                        for tt in range(4):
                            j = qb * 4 + tt
                            xi_ = (cb * 4 + tt) % 2
                            k.dma("sp", xin[xi_][:], xsrc[j * 128:(j + 1) * 128, cb * 512:(cb + 1) * 512], writes=[txin[xi_]])
                            k.op("dve", lambda e: e.tensor_tensor(out=xo[xi_][:], in0=ps[:, banks[tt], :], in1=xin[xi_][:], op=ALU.add),
                                 reads=[tpb[banks[tt]], txin[xi_]], writes=[txo[xi_]])
                            k.dma("sp", xres[j * 128:(j + 1) * 128, cb * 512:(cb + 1) * 512], xo[xi_][:], reads=[txo[xi_]])
                k.barrier()
        with ExitStack() as sf:
            fxt = [sb(sf, "fxt%d" % i, [128, D]) for i in range(2)]
            fyo = [sb(sf, "fyo%d" % i, [128, D]) for i in range(2)]
            fw_ = sb(sf, "ffw", [128, D])
            fss = [sb(sf, "fss%d" % i, [128, 1]) for i in range(2)]
            fhj = [sb(sf, "fjunk%d" % i, [128, D], BF16) for i in range(2)]
            tfxt, tfyo, tfss, tfhj = [[T(), T()] for _ in range(4)]
            tfw = T()
            k.dma("sp", fw_[:], bass.AP(final_norm, 0, [[0, 128], [1, D]]), writes=[tfw])
            for j in range(NT):
                i_ = j % 2
                k.dma("sp", fxt[i_][:], xres[j * 128:(j + 1) * 128, :], writes=[tfxt[i_]])
                k.op("act", lambda e: e.activation(out=fhj[i_][:], in_=fxt[i_][:], func=AF.Square, accum_out=fss[i_][:, 0:1]),
                     reads=[tfxt[i_]], writes=[tfhj[i_], tfss[i_]])
                k.op("act", lambda e: e.activation(out=fss[i_][:, 0:1], in_=fss[i_][:, 0:1], func=AF.Ln, scale=1.0 / D, bias=EPSC),
                     reads=[tfss[i_], tcst], writes=[tfss[i_]])
                k.op("act", lambda e: e.activation(out=fss[i_][:, 0:1], in_=fss[i_][:, 0:1], func=AF.Exp, scale=-0.5),
                     reads=[tfss[i_]], writes=[tfss[i_]])
                k.op("dve", lambda e: e.scalar_tensor_tensor(out=fyo[i_][:], in0=fxt[i_][:], scalar=fss[i_][:, 0:1], in1=fw_[:],
                                                              op0=ALU.mult, op1=ALU.mult), reads=[tfxt[i_], tfss[i_], tfw], writes=[tfyo[i_]])
                k.dma("sp", out_d[j * 128:(j + 1) * 128, :], fyo[i_][:], reads=[tfyo[i_]])
            k.barrier()
        print("kernel instructions:", k.n_inst)
    return nc


def make_consts(TOK, tok0, sel_f, sel_b):
    t = np.arange(tok0, tok0 + TOK)
    row = (t // 64).astype(np.float32)
    col = (t % 64).astype(np.float32)
    inv = (np.float32(10000.0) ** (-np.arange(0, 64, 2, dtype=np.float32) / np.float32(64))).astype(np.float32)
    ar = row[:, None] * inv[None, :]
    ac = col[:, None] * inv[None, :]
    cr, sr, cc_, sc = np.cos(ar), np.sin(ar), np.cos(ac), np.sin(ac)
    rope = np.concatenate([cr, cr, cc_, cc_, -sr, sr, -sc, sc], axis=1).astype(np.float32)
    cst = np.zeros((128, 1024), np.float32)
    cst[:, 0:128] = np.eye(128, dtype=np.float32)
    q = np.arange(128, dtype=np.float32)
    cst[:, 128:256] = (q + 1)[None, :]
    cst[:, 256:384] = (128 - q)[None, :]
    kk = q[:, None]
    cst[:, 384:512] = np.maximum(q[None, :] - kk, 0)
    cst[:, 512:640] = np.maximum(kk - q[None, :], 0)
    cst[:, 640] = 127 - q
    cst[:, 641] = q
    cst[:, 642] = sel_f
    cst[:, 643] = sel_b
    cst[:, 644] = 1e-6
    return rope, cst


def run(inputs, L, NP, S, B):
    TOK = S // NP
    ncores = B * NP
    nc = build(L, TOK, NP)
    x = np.asarray(inputs["x"], np.float32)
    rdec = np.concatenate([np.asarray(inputs["ret_decay_fwd"], np.float32),
                           np.asarray(inputs["ret_decay_bwd"], np.float32)], axis=1)
    shared = {
        "w_in": np.ascontiguousarray(inputs["w_in"], dtype=np.float32),
        "w_out": np.ascontiguousarray(inputs["w_out"], dtype=np.float32),
        "norm_w": np.ascontiguousarray(inputs["norm_w"], dtype=np.float32),
        "q_norm": np.ascontiguousarray(inputs["q_norm"], dtype=np.float32),
        "k_norm": np.ascontiguousarray(inputs["k_norm"], dtype=np.float32),
        "rdec": np.ascontiguousarray(rdec),
        "ret_norm": np.ascontiguousarray(np.asarray(inputs["ret_norm"], np.float32).reshape(L, 2048)),
        "final_norm": np.ascontiguousarray(np.asarray(inputs["final_norm"], np.float32).reshape(1, D)),
    }
    in_maps = []
    for b in range(B):
        for r in range(NP):
            rope, cst = make_consts(TOK, r * TOK, 1.0 if (NP == 2 and r == 1) else 0.0, 1.0 if (NP == 2 and r == 0) else 0.0)
            m = dict(shared)
            m["x"] = np.ascontiguousarray(x[b, r * TOK:(r + 1) * TOK, :])
            m["rope"] = rope
            m["cst"] = cst
            in_maps.append(m)
    res = run_bass_kernel_spmd(nc, in_maps, core_ids=list(range(ncores)))
    out = np.zeros((B, S, D), np.float32)
    for b in range(B):
        for r in range(NP):
            out[b, r * TOK:(r + 1) * TOK, :] = res.results[b * NP + r]["out"]
    return out


NP_CFG = 2


def kernel(x, norm_w, w_in, q_norm, k_norm, ret_decay_fwd, ret_decay_bwd, ret_norm, w_out, final_norm):
    inputs = dict(x=x, norm_w=norm_w, w_in=w_in, q_norm=q_norm, k_norm=k_norm, ret_decay_fwd=ret_decay_fwd,
                  ret_decay_bwd=ret_decay_bwd, ret_norm=ret_norm, w_out=w_out, final_norm=final_norm)
    return run(inputs, L=4, NP=NP_CFG, S=4096, B=4)
```

```python
import numpy as np
from contextlib import ExitStack
import concourse.bass as bass
import concourse.mybir as mybir
from concourse.bass_utils import run_bass_kernel_spmd

F32 = mybir.dt.float32
BF16 = mybir.dt.bfloat16
AF = mybir.ActivationFunctionType
ALU = mybir.AluOpType
AX = mybir.AxisListType

D = 4096
KC = 32
INW = 11264
EPS = 1e-6
NDS = 24

BLOCK_ORDER = [("aq", 0), ("aq", 1), ("aq", 2), ("aq", 3), ("ak", 4), ("av", 5),
               ("ag", 6), ("ag", 7), ("ag", 8), ("ag", 9), ("rq", 10), ("rq", 11),
               ("rv", 14), ("rv", 15), ("rv", 16), ("rv", 17), ("rk", 12), ("rk", 13),
               ("rg", 18), ("rg", 19), ("rg", 20), ("rg", 21)]


class T:
    __slots__ = ("name", "w", "r", "ex")

    def __init__(self, name="", ex=False):
        self.name = name
        self.ex = ex
        self.w = None
        self.r = []


class K:
    def __init__(self, nc, es):
        self.nc = nc
        self.E = {"pe": nc.tensor, "act": nc.scalar, "dve": nc.vector, "pool": nc.gpsimd, "sp": nc.sync}
        self.sems = {}
        self.cnt = {}
        for e in ("pe", "act", "dve", "pool"):
            self.sems[e] = es.enter_context(nc.semaphore("p_" + e))
            self.cnt[e] = 0
        for q in ("sp", "pool"):
            for i in range(NDS):
                self.sems[(q, i)] = es.enter_context(nc.semaphore("d%s%d" % (q, i)))
                self.cnt[(q, i)] = 0
        self.dnext = {"sp": 0, "pool": 0}
        self.waited = {e: {} for e in self.E}
        self.n_inst = 0

    def _wait(self, eng, ev):
        if ev is None:
            return
        key, val = ev
        if key == eng and eng == "pe":
            return
        if self.waited[eng].get(key, 0) >= val:
            return
        self.waited[eng][key] = val
        self.E[eng].wait_ge(self.sems[key], val)
        self.n_inst += 1

    def _deps(self, eng, reads, writes):
        for t in reads:
            self._wait(eng, t.w)
        for t in writes:
            self._wait(eng, t.w)
            for ev in t.r:
                self._wait(eng, ev)

    def _record(self, ev, reads, writes):
        for t in reads:
            t.r.append(ev)
            if len(t.r) > 48:
                best = {}
                for k_, v_ in t.r:
                    if best.get(k_, 0) < v_:
                        best[k_] = v_
                t.r = list(best.items())
        for t in writes:
            t.w = ev
            t.r = []

    def op(self, eng, fn, reads=(), writes=(), inc=True):
        exr = [t for t in reads if t.ex]
        if exr:
            writes = list(writes) + [t for t in exr if t not in writes]
            reads = [t for t in reads if not t.ex]
        self._deps(eng, reads, writes)
        ins = fn(self.E[eng])
        self.n_inst += 1
        if inc:
            self.cnt[eng] += 1
            ins.then_inc(self.sems[eng], 1)
            ev = (eng, self.cnt[eng])
        else:
            ev = (eng, self.cnt[eng] + 1)
        self._record(ev, reads, writes)
        return ev

    def dma(self, q, out, in_, reads=(), writes=(), **kw):
        self._deps(q, reads, writes)
        i = self.dnext[q]
        self.dnext[q] = (i + 1) % NDS
        key = (q, i)
        self._wait(q, (key, self.cnt[key]))
        ins = self.E[q].dma_start(out=out, in_=in_, **kw)
        self.n_inst += 1
        self.cnt[key] += 16
        ins.then_inc(self.sems[key], 16)
        ev = (key, self.cnt[key])
        self._record(ev, reads, writes)
        return ev

    def barrier(self):
        for eng in self.E:
            for key, c in self.cnt.items():
                if c > 0:
                    self._wait(eng, (key, c))


def fap(base, pat):
    a = base.ap
    return bass.AP(base.tensor, base.offset, [list(a[0])] + [list(p) for p in pat])


import os
STOP = os.environ.get("KSTOP", "")


def build(L, TOK, NP, x_from_input=True):
    NT = TOK // 128
    NTB = TOK // 512
    SF = TOK * NP
    NKT = SF // 128
    nc = bass.Bass("TRN2", target_bir_lowering=False)

    def din(name, shape, dt=F32):
        return nc.dram_tensor(name, shape, dt, kind="ExternalInput")

    x_in = din("x", [TOK, D]).ap()
    w_in = din("w_in", [L, D, INW])
    w_out = din("w_out", [L, D, D])
    norm_w = din("norm_w", [L, D])
    q_norm = din("q_norm", [L, 128])
    k_norm = din("k_norm", [L, 128])
    rdec = din("rdec", [L, 16])
    ret_norm = din("ret_norm", [L, 2048])
    final_norm = din("final_norm", [1, D])
    rope_d = din("rope", [TOK, 256]).ap()
    cst_d = din("cst", [128, 648]).ap()
    out_d = nc.dram_tensor("out", [TOK, D], F32, kind="ExternalOutput").ap()

    def dsc(name, shape, dt):
        return nc.dram_tensor(name, shape, dt).ap()

    wibs = [dsc("wib%d" % l_, [22 * 128, 16384], BF16) for l_ in range(L)]
    wobs = [dsc("wob%d" % l_, [8 * 128, 16384], BF16) for l_ in range(L)]
    xres = dsc("xres", [TOK, D], F32)
    qT = dsc("qT", [16 * 128, TOK], BF16)
    kx_loc = dsc("kx_loc", [512, TOK], BF16)
    vx_loc = dsc("vx_loc", [512, NT * 128], BF16)
    if NP > 1:
        kx_all = dsc("kx_all", [NP * 512, TOK], BF16)
        vx_all = dsc("vx_all", [NP * 512, NT * 128], BF16)
        send_all = dsc("send_all", [NP * 128, 4096], F32)
    agT = dsc("agT", [2048, TOK], BF16)
    rqT = dsc("rqT", [1024, TOK], BF16)
    rkT = dsc("rkT", [1024, TOK], BF16)
    rv_d = dsc("rv_d", [TOK, 2048], BF16)
    rg_d = dsc("rg_d", [TOK, 2048], BF16)
    U_d = dsc("U_d", [NT * 128, 4096], F32)
    send_loc = dsc("send_loc", [128, 4096], F32)
    Sst = dsc("Sst", [NT * 128, 4096], BF16)
    if NP == 1:
        kx_all, vx_all, send_all = kx_loc, vx_loc, send_loc

    with ExitStack() as es:
        k = K(nc, es)
        cc_sem = es.enter_context(nc.semaphore("cc"))
        cc_cnt = [0]
        ps = es.enter_context(nc.psum_tensor("ps", [128, 8, 512], F32))
        tpb = [T("bank%d" % i, ex=True) for i in range(8)]

        sbn = [0]

        def sb(st, name, shape, dt=F32):
            sbn[0] += 1
            return st.enter_context(nc.sbuf_tensor("s%d_%s" % (sbn[0], name), shape, dt))

        cst = sb(es, "cst", [128, 648])
        ident16 = sb(es, "ident16", [128, 128], BF16)
        ones16 = sb(es, "ones16", [128, 128], BF16)
        gn = sb(es, "gn", [128, 2048])
        nwT = sb(es, "nwT", [128, 32])
        qkg = sb(es, "qkg", [128, 256])
        dec = sb(es, "dec", [128, 16])
        lg = sb(es, "lg", [128, 16])
        ptmp = sb(es, "ptmp", [128, 16])
        zeta = sb(es, "zeta", [128, 16])
        gC = sb(es, "gC", [128, 16])
        gCj = sb(es, "gCj", [128, NT, 8])
        etmp = sb(es, "etmp", [128, 128])
        nw32 = sb(es, "nw32", [32, 128])
        tnw32 = T()
        tcst, tid, tones, tgn, tDT, txi, tnw, tqkg, tdec, tlg, tptmp, tzeta, tgC, tgCj, tetmp = [T() for _ in range(15)]

        IDF = cst[:, 0:128]
        QP1 = cst[:, 128:256]
        CMQ = cst[:, 256:384]
        DPOS = cst[:, 384:512]
        DNEG = cst[:, 512:640]
        CM1P = cst[:, 640:641]
        PIDX = cst[:, 641:642]
        SELF = cst[:, 642:643]
        SELB = cst[:, 643:644]
        EPSC = cst[:, 644:645]

        k.dma("sp", cst[:], cst_d, writes=[tcst])
        k.op("dve", lambda e: e.tensor_copy(out=ident16[:], in_=IDF), reads=[tcst], writes=[tid])
        k.op("dve", lambda e: e.memset(ones16[:], 1.0), writes=[tones])

        twi = [[T("wi%d_%d" % (l_, b_)) for b_ in range(22)] for l_ in range(L)]
        two_ = [[T("wo%d_%d" % (l_, b_)) for b_ in range(8)] for l_ in range(L)]
        cast_q = []
        for l_ in range(L):
            wv_i = w_in.ap()[l_].rearrange("(c p) n -> p c n", p=128)
            for (_kd, b_) in BLOCK_ORDER:
                dst = wibs[l_][b_ * 128:(b_ + 1) * 128, :].rearrange("p (c n) -> p c n", n=512)
                for c8 in range(4):
                    cast_q.append((dst[:, c8 * 8:(c8 + 1) * 8, :], wv_i[:, c8 * 8:(c8 + 1) * 8, b_ * 512:(b_ + 1) * 512], twi[l_][b_]))
            wv_o = w_out.ap()[l_].rearrange("(c p) n -> p c n", p=128)
            for b_ in range(8):
                dst = wobs[l_][b_ * 128:(b_ + 1) * 128, :].rearrange("p (c n) -> p c n", n=512)
                for c8 in range(4):
                    cast_q.append((dst[:, c8 * 8:(c8 + 1) * 8, :], wv_o[:, c8 * 8:(c8 + 1) * 8, b_ * 512:(b_ + 1) * 512], two_[l_][b_]))
        cast_pos = [0]
        cast_last = {}
        for i_, (_d, _s, t_) in enumerate(cast_q):
            cast_last[id(t_)] = i_

        def need_cast(t_):
            while cast_pos[0] <= cast_last[id(t_)]:
                pump(1)

        def pump(n):
            for _ in range(n):
                if cast_pos[0] < len(cast_q):
                    d_, s_, t_ = cast_q[cast_pos[0]]
                    cast_pos[0] += 1
                    k.dma("pool", d_, s_, writes=[t_])

        pump(88)

        def rstd_chain(v_ap, tv, scale):
            k.op("dve", lambda e: e.tensor_scalar(out=v_ap, in0=v_ap, scalar1=scale, scalar2=EPS,
                                                   op0=ALU.mult, op1=ALU.add), reads=[tv], writes=[tv])
            k.op("act", lambda e: e.activation(out=v_ap, in_=v_ap, func=AF.Ln), reads=[tv], writes=[tv])
            k.op("act", lambda e: e.activation(out=v_ap, in_=v_ap, func=AF.Exp, scale=-0.5), reads=[tv], writes=[tv])

        for l in range(L):
            xsrc = x_in if (l == 0 and x_from_input) else xres
            k.dma("sp", gn[:], bass.AP(ret_norm, l * 2048, [[0, 128], [1, 2048]]), writes=[tgn])
            k.dma("sp", qkg[:, 0:128], bass.AP(q_norm, l * 128, [[0, 128], [1, 128]]), writes=[tqkg])
            k.dma("sp", qkg[:, 128:256], bass.AP(k_norm, l * 128, [[0, 128], [1, 128]]), writes=[tqkg])
            k.op("dve", lambda e: e.tensor_scalar(out=qkg[:, 0:128], in0=qkg[:, 0:128], scalar1=float(128 ** -0.5),
                                                   scalar2=0.0, op0=ALU.mult, op1=ALU.add), reads=[tqkg], writes=[tqkg])
            k.dma("sp", dec[:], bass.AP(rdec, l * 16, [[0, 128], [1, 16]]), writes=[tdec])
            k.dma("sp", nw32[0:32, :], bass.AP(norm_w, l * D, [[128, 32], [1, 128]]), writes=[tnw32])
            k.op("pe", lambda e: e.matmul(ps[:, 4, 0:32], nw32[0:32, :], cst[0:32, 0:32], start=True, stop=True),
                 reads=[tnw32, tcst], writes=[tpb[4]])
            k.op("dve", lambda e: e.tensor_copy(out=nwT[:], in_=ps[:, 4, 0:32]), reads=[tpb[4]], writes=[tnw])
            k.op("act", lambda e: e.activation(out=lg[:], in_=dec[:], func=AF.Exp, scale=-1.0), reads=[tdec], writes=[tlg])
            k.op("dve", lambda e: e.tensor_scalar(out=ptmp[:], in0=lg[:], scalar1=1.0 / 7, scalar2=-1.0 / 6,
                                                   op0=ALU.mult, op1=ALU.add), reads=[tlg], writes=[tptmp])
            for cc in (1.0 / 5, -1.0 / 4, 1.0 / 3, -1.0 / 2, 1.0):
                k.op("dve", lambda e: e.tensor_tensor(out=ptmp[:], in0=ptmp[:], in1=lg[:], op=ALU.mult),
                     reads=[tptmp, tlg], writes=[tptmp])
                k.op("dve", lambda e: e.tensor_scalar(out=ptmp[:], in0=ptmp[:], scalar1=1.0, scalar2=cc,
                                                       op0=ALU.mult, op1=ALU.add), reads=[tptmp], writes=[tptmp])
            k.op("dve", lambda e: e.tensor_tensor(out=ptmp[:], in0=ptmp[:], in1=lg[:], op=ALU.mult),
                 reads=[tptmp, tlg], writes=[tptmp])
            k.op("dve", lambda e: e.tensor_scalar(out=lg[:], in0=ptmp[:], scalar1=-1.0, scalar2=0.0,
                                                   op0=ALU.mult, op1=ALU.add), reads=[tptmp], writes=[tlg])
            k.op("act", lambda e: e.activation(out=zeta[:, 0:8], in_=lg[:, 0:8], func=AF.Exp, scale=CM1P),
                 reads=[tcst, tlg], writes=[tzeta])
            k.op("act", lambda e: e.activation(out=zeta[:, 8:16], in_=lg[:, 8:16], func=AF.Exp, scale=PIDX),
                 reads=[tcst, tlg], writes=[tzeta])
            k.op("act", lambda e: e.activation(out=gC[:], in_=lg[:], func=AF.Exp, scale=128.0), reads=[tlg], writes=[tgC])
            for j in range(NT):
                k.op("act", lambda e: e.activation(out=gCj[:, j, :], in_=lg[:, 8:16], func=AF.Exp, scale=128.0 * j),
                     reads=[tlg], writes=[tgCj])

            if STOP == "tables":
                k.barrier()
                break
            with ExitStack() as sa:
                hT = sb(sa, "hT", [128, 32, 512], BF16)
                wt = [sb(sa, "wt%d" % i, [128, 16, 512], BF16) for i in range(2)]
                xt2 = [sb(sa, "xt%d" % i, [128, D]) for i in range(2)]
                h162 = [sb(sa, "h16%d" % i, [128, D], BF16) for i in range(2)]
                txt2 = [T(), T()]
                th162 = [T(), T()]
                ss2 = [sb(sa, "ss%d" % i, [128, 1]) for i in range(2)]
                tss2 = [T(), T()]
                rvs = sb(sa, "rvs", [128, 4, 2048], BF16)
                ust = [sb(sa, "ust%d" % i, [128, 4, 256]) for i in range(2)]
                send = sb(sa, "send", [128, 2, 8, 256])
                ropet = sb(sa, "ropet", [128, 4, 256])
                ssq4 = [sb(sa, "ssq%d" % i, [128, 4]) for i in range(4)]
                xn4 = [sb(sa, "xn4_%d" % i, [128, 512]) for i in range(4)]
                sq = sb(sa, "sq", [128, 512])
                tmpA = sb(sa, "tmpA", [128, 512])
                tmpT = sb(sa, "tmpT", [128, 512])
                xrb = [[sb(sa, "xr%d_%d" % (p_, t_), [128, 512], BF16) for t_ in range(4)] for p_ in range(2)]
                txrb = [[T() for t_ in range(4)] for p_ in range(2)]
                rkzb = [sb(sa, "rkz%d" % t_, [128, 2, 512], BF16) for t_ in range(4)]
                trkzb = [T() for t_ in range(4)]
                deferred = []
                blk_i = [0]

                chain2s = []

                def run_chain2():
                    for f_ in chain2s:
                        f_()
                    del chain2s[:]

                def flush():
                    for f_ in deferred:
                        f_()
                    del deferred[:]
                stg = [sb(sa, "stg%d" % i, [128, 512], BF16) for i in range(4)]
                thT, tsend, tropet, tsq, ttA, ttT = [T() for _ in range(6)]
                trvs4 = [[T() for _ in range(4)] for _ in range(4)]
                tssq4 = [T() for _ in range(4)]
                txn4 = [T() for _ in range(4)]
                twt = [T(), T()]
                tust = [T(), T()]
                tstg = [T() for _ in range(4)]
                stg_i = [0]
                evac_i = [0]
                acc_i = [0]

                def next_stg():
                    i = stg_i[0] % 4
                    stg_i[0] += 1
                    return stg[i], tstg[i]

                def evac_copy(out_ap, in_ap, reads, writes, scale=None):
                    evac_i[0] += 1
                    if False:
                        k.op("dve", lambda e: e.tensor_copy(out=out_ap, in_=in_ap), reads=reads, writes=writes)
                    else:
                        if scale is None:
                            k.op("act", lambda e: e.activation(out=out_ap, in_=in_ap, func=AF.Copy), reads=reads, writes=writes)
                        else:
                            k.op("act", lambda e: e.activation(out=out_ap, in_=in_ap, func=AF.Copy, scale=scale),
                                 reads=reads, writes=writes)

                k.op("dve", lambda e: e.memset(send[:], 0.0), writes=[tsend])

                def rope(src, tsrc, dst, tdst, tt):
                    C_ = fap(ropet[:, tt, 0:128], [[0, 4], [1, 128]])
                    k.op("dve", lambda e: e.tensor_tensor(out=fap(tmpA[:], [[128, 4], [1, 128]]),
                                                           in0=fap(src, [[128, 4], [1, 128]]), in1=C_, op=ALU.mult),
                         reads=[tsrc, tropet], writes=[ttA])
                    pat = [[128, 4], [64, 2], [1, 32]]
                    spat = [[0, 4], [64, 2], [1, 32]]
                    k.op("dve", lambda e: e.tensor_tensor(out=fap(tmpT[:, 0:32], pat), in0=fap(src[:, 32:64], pat),
                                                           in1=fap(ropet[:, tt, 128:160], spat), op=ALU.mult),
                         reads=[tsrc, tropet], writes=[ttT])
                    k.op("dve", lambda e: e.tensor_tensor(out=fap(tmpT[:, 32:64], pat), in0=fap(src[:, 0:32], pat),
                                                           in1=fap(ropet[:, tt, 160:192], spat), op=ALU.mult),
                         reads=[tsrc, tropet], writes=[ttT])
                    k.op("dve", lambda e: e.tensor_tensor(out=dst, in0=tmpA[:], in1=tmpT[:], op=ALU.add),
                         reads=[ttA, ttT], writes=[tdst])

                def transposes_out(src16, tsrc, dram_rows, j):
                    for c4 in range(4):
                        k.op("pe", lambda e: e.matmul(ps[:, 4, c4 * 128:(c4 + 1) * 128], src16[:, c4 * 128:(c4 + 1) * 128],
                                                       ident16[:], start=True, stop=True),
                             reads=[tsrc, tid], writes=[tpb[4]], inc=(c4 == 3))
                    s_, ts_ = next_stg()
                    evac_copy(s_[:], ps[:, 4, :], [tpb[4]], [ts_])
                    k.dma("sp", dram_rows[:, j * 128:(j + 1) * 128].rearrange("(h d) t -> d h t", d=128),
                          fap(s_[:], [[128, 4], [1, 128]]), reads=[ts_])

                witems = [(b_, half_) for _tb in range(NTB) for (_kd, b_) in BLOCK_ORDER for half_ in range(2)]

                def emit_wload(n_):
                    if n_ < len(witems):
                        b_, half_ = witems[n_]
                        pump(1)
                        need_cast(twi[l][b_])
                        k.dma("sp", wt[n_ % 2][:], wibs[l][b_ * 128:(b_ + 1) * 128, half_ * 8192:(half_ + 1) * 8192]
                              .rearrange("p (c n) -> p c n", n=512), reads=[twi[l][b_]], writes=[twt[n_ % 2]])

                emit_wload(0)
                emit_wload(1)
                for tb in range(NTB):
                    if tb == 0:
                        k.dma("sp", ropet[:], rope_d[tb * 512:(tb + 1) * 512, :].rearrange("(j p) c -> p j c", p=128),
                              writes=[tropet])
                    for tt in range(4):
                        j = tb * 4 + tt
                        xt, txt, h16, th16, ss, tss = xt2[j % 2], txt2[j % 2], h162[j % 2], th162[j % 2], ss2[j % 2], tss2[j % 2]
                        if tb == 0 or tt >= 2:
                            k.dma("sp", xt[:], xsrc[j * 128:(j + 1) * 128, :], writes=[txt])
                        k.op("act", lambda e: e.activation(out=h16[:], in_=xt[:], func=AF.Square, accum_out=ss[:, 0:1]),
                             reads=[txt], writes=[th16, tss])
                        k.op("act", lambda e: e.activation(out=ss[:, 0:1], in_=ss[:, 0:1], func=AF.Ln, scale=1.0 / D, bias=EPSC),
                             reads=[tss, tcst], writes=[tss])
                        k.op("act", lambda e: e.activation(out=ss[:, 0:1], in_=ss[:, 0:1], func=AF.Exp, scale=-0.5), reads=[tss], writes=[tss])
                        k.op("dve", lambda e: e.tensor_scalar(out=h16[:], in0=xt[:], scalar1=ss[:, 0:1], scalar2=0.0,
                                                               op0=ALU.mult, op1=ALU.add), reads=[txt, tss], writes=[th16])
                        for g in range(8):
                            bank = 4 + (g % 2)
                            for c4 in range(4):
                                c = g * 4 + c4
                                k.op("pe", lambda e: e.matmul(ps[:, bank, c4 * 128:(c4 + 1) * 128], h16[:, c * 128:(c + 1) * 128],
                                                               ident16[:], start=True, stop=True),
                                     reads=[th16, tid], writes=[tpb[bank]], inc=(c4 == 3))
                            k.op("dve", lambda e: e.tensor_tensor(out=hT[:, g * 4:(g + 1) * 4, tt * 128:(tt + 1) * 128],
                                                                   in0=fap(ps[:, bank, :], [[128, 4], [1, 128]]),
                                                                   in1=fap(nwT[:, g * 4:(g + 1) * 4], [[1, 4], [0, 128]]),
                                                                   op=ALU.mult),
                                 reads=[tpb[bank], tnw], writes=[thT])
                    for bpos, (kind, b) in enumerate(BLOCK_ORDER):
                        if bpos == 18 and tb + 1 < NTB:
                            k.dma("sp", ropet[:], rope_d[(tb + 1) * 512:(tb + 2) * 512, :].rearrange("(j p) c -> p j c", p=128),
                                  writes=[tropet])
                            for t2 in range(2):
                                jn = (tb + 1) * 4 + t2
                                k.dma("sp", xt2[jn % 2][:], xsrc[jn * 128:(jn + 1) * 128, :], writes=[txt2[jn % 2]])
                        wrow = b * 128
                        wib = wibs[l]
                        if kind != "ag":
                            banks = [0, 1, 2, 3]
                            for half in range(2):
                                wi = acc_i[0] % 2
                                for tt in range(4):
                                    for c in range(16):
                                        cg = half * 16 + c
                                        k.op("pe", lambda e: e.matmul(ps[:, banks[tt], :], hT[:, cg, tt * 128:(tt + 1) * 128],
                                                                       wt[wi][:, c, :], start=(cg == 0), stop=(cg == 31)),
                                             reads=[thT, twt[wi]], writes=[tpb[banks[tt]]], inc=(c == 15))
                                emit_wload(acc_i[0] + 2)
                                acc_i[0] += 1
                            flush()
                            par = blk_i[0] % 2
                            blk_i[0] += 1
                            for tt in range(4):
                                j = tb * 4 + tt
                                pb = ps[:, banks[tt], :]
                                tb_ = tpb[banks[tt]]
                                xr = xrb[par][tt]
                                txr = txrb[par][tt]
                                rkz = rkzb[tt]
                                trkz = trkzb[tt]
                                xn = xn4[tt]
                                txn = txn4[tt]
                                if kind in ("aq", "ak"):
                                    goff = 0 if kind == "aq" else 128
                                    for hh in range(4):
                                        k.op("act", lambda e: e.activation(out=sq[:, hh * 128:(hh + 1) * 128], in_=pb[:, hh * 128:(hh + 1) * 128],
                                                                            func=AF.Square, accum_out=ssq4[tt][:, hh:hh + 1]),
                                             reads=[tb_], writes=[tsq, tssq4[tt]])
                                    k.op("act", lambda e: e.activation(out=ssq4[tt][:], in_=ssq4[tt][:], func=AF.Ln, scale=1.0 / 128, bias=EPSC),
                                         reads=[tssq4[tt], tcst], writes=[tssq4[tt]])
                                    k.op("act", lambda e: e.activation(out=ssq4[tt][:], in_=ssq4[tt][:], func=AF.Exp, scale=-0.5),
                                         reads=[tssq4[tt]], writes=[tssq4[tt]])
                                    k.op("dve", lambda e: e.tensor_tensor(out=fap(xn[:], [[128, 4], [1, 128]]),
                                                                           in0=fap(pb, [[128, 4], [1, 128]]),
                                                                           in1=fap(ssq4[tt][:], [[1, 4], [0, 128]]), op=ALU.mult),
                                         reads=[tb_, tssq4[tt]], writes=[txn])
                                    rows_ = qT[b * 512:(b + 1) * 512, :] if kind == "aq" else kx_loc

                                    def chain2(xn=xn, txn=txn, xr=xr, txr=txr, tt=tt, goff=goff, rows_=rows_, j=j):
                                        k.op("dve", lambda e: e.tensor_tensor(out=fap(xn[:], [[128, 4], [1, 128]]),
                                                                               in0=fap(xn[:], [[128, 4], [1, 128]]),
                                                                               in1=fap(qkg[:, goff:goff + 128], [[0, 4], [1, 128]]),
                                                                               op=ALU.mult),
                                             reads=[txn, tqkg], writes=[txn])
                                        rope(xn[:], txn, xr[:], txr, tt)
                                        deferred.append(lambda xr_=xr, t_=txr, r_=rows_, j_=j: transposes_out(xr_, t_, r_, j_))
                                    chain2s.append(chain2)
                                elif kind == "av":
                                    s_, ts_ = next_stg()
                                    evac_copy(s_[:], pb, [tb_], [ts_])
                                    k.dma("sp", vx_loc[:, j * 128:(j + 1) * 128].rearrange("(g p) d -> p g d", p=128),
                                          fap(s_[:], [[128, 4], [1, 128]]), reads=[ts_])
                                elif kind == "rq":
                                    evac_copy(xn[:], pb, [tb_], [txn])
                                    rope(xn[:], txn, xr[:], txr, tt)
                                    hb = b - 10
                                    deferred.append(lambda xr_=xr, t_=txr, r_=rqT[hb * 512:(hb + 1) * 512, :], j_=j: transposes_out(xr_, t_, r_, j_))
                                elif kind == "rk":
                                    evac_copy(xn[:], pb, [tb_], [txn], scale=float(128 ** -0.5))
                                    rope(xn[:], txn, xr[:], txr, tt)
                                    hb = b - 12
                                    deferred.append(lambda xr_=xr, t_=txr, r_=rkT[hb * 512:(hb + 1) * 512, :], j_=j: transposes_out(xr_, t_, r_, j_))
                                    for dr in range(2):
                                        zoff = dr * 8 + hb * 4
                                        k.op("dve", lambda e: e.tensor_tensor(out=fap(rkz[:, dr, :], [[128, 4], [1, 128]]),
                                                                               in0=fap(xr[:], [[128, 4], [1, 128]]),
                                                                               in1=fap(zeta[:, zoff:zoff + 4], [[1, 4], [0, 128]]),
                                                                               op=ALU.mult),
                                             reads=[txr, tzeta], writes=[trkz])

                                    def u_part(rkz=rkz, trkz=trkz, tt=tt, j=j, hb=hb):
                                        for dr in range(2):
                                            bb = 6 if dr == 0 else 4
                                            for hh in range(4):
                                                h = hb * 4 + hh
                                                bank = bb + hh // 2
                                                k.op("pe", lambda e: e.matmul(ps[:, bank, (hh % 2) * 256:(hh % 2 + 1) * 256],
                                                                               rkz[:, dr, hh * 128:(hh + 1) * 128],
                                                                               rvs[:, tt, h * 256:(h + 1) * 256], start=True, stop=True),
                                                     reads=[trkz, trvs4[tt][h // 2]], writes=[tpb[bank]], inc=(hh % 2 == 1))
                                        for dr in range(2):
                                            bb = 6 if dr == 0 else 4
                                            u_ = ust[dr]
                                            tu_ = tust[dr]
                                            k.op("act", lambda e: e.activation(out=fap(u_[:, 0:2, :], [[1, 512]]), in_=ps[:, bb, :], func=AF.Copy),
                                                 reads=[tpb[bb]], writes=[tu_])
                                            k.op("dve", lambda e: e.tensor_copy(out=fap(u_[:, 2:4, :], [[1, 512]]), in_=ps[:, bb + 1, :]),
                                                 reads=[tpb[bb + 1]], writes=[tu_])
                                            k.dma("sp", U_d[j * 128:(j + 1) * 128, dr * 2048 + hb * 1024: dr * 2048 + (hb + 1) * 1024],
                                                  fap(u_[:], [[1, 1024]]), reads=[tu_])
                                            for hh in range(4):
                                                h = hb * 4 + hh
                                                if dr == 0:
                                                    k.op("dve", lambda e: e.scalar_tensor_tensor(out=send[:, 0, h, :], in0=send[:, 0, h, :],
                                                                                                  scalar=gC[:, h:h + 1], in1=u_[:, hh, :],
                                                                                                  op0=ALU.mult, op1=ALU.add),
                                                         reads=[tu_, tgC], writes=[tsend])
                                                else:
                                                    k.op("dve", lambda e: e.scalar_tensor_tensor(out=send[:, 1, h, :], in0=u_[:, hh, :],
                                                                                                  scalar=gCj[:, j, h:h + 1], in1=send[:, 1, h, :],
                                                                                                  op0=ALU.mult, op1=ALU.add),
                                                         reads=[tu_, tgCj], writes=[tsend])
                                    deferred.append(u_part)
                                elif kind == "rv":
                                    hb = b - 14
                                    evac_copy(rvs[:, tt, hb * 512:(hb + 1) * 512], pb, [tb_], [trvs4[tt][hb]])
                                    k.dma("sp", rv_d[j * 128:(j + 1) * 128, hb * 512:(hb + 1) * 512],
                                          rvs[:, tt, hb * 512:(hb + 1) * 512], reads=[trvs4[tt][hb]])
                                elif kind == "rg":
                                    hb = b - 18
                                    k.op("act", lambda e: e.activation(out=sq[:], in_=pb, func=AF.Silu), reads=[tb_], writes=[tsq])
                                    s_, ts_ = next_stg()
                                    k.op("dve", lambda e: e.tensor_tensor(out=s_[:], in0=sq[:], in1=gn[:, hb * 512:(hb + 1) * 512],
                                                                           op=ALU.mult), reads=[tsq, tgn], writes=[ts_])
                                    k.dma("sp", rg_d[j * 128:(j + 1) * 128, hb * 512:(hb + 1) * 512], s_[:], reads=[ts_])
                            run_chain2()
                        else:
                            hb = b - 6
                            for half in range(2):
                                wi = acc_i[0] % 2
                                for cb in range(4):
                                    for c in range(16):
                                        cg = half * 16 + c
                                        k.op("pe", lambda e: e.matmul(ps[:, cb, :], wt[wi][:, c, cb * 128:(cb + 1) * 128],
                                                                       hT[:, cg, :], start=(cg == 0), stop=(cg == 31)),
                                             reads=[thT, twt[wi]], writes=[tpb[cb]], inc=(c == 15))
                                emit_wload(acc_i[0] + 2)
                                acc_i[0] += 1
                            flush()
                            for cb in range(4):
                                s_, ts_ = next_stg()
                                k.op("act", lambda e: e.activation(out=s_[:], in_=ps[:, cb, :], func=AF.Silu),
                                     reads=[tpb[cb]], writes=[ts_])
                                r0 = hb * 512 + cb * 128
                                k.dma("sp", agT[r0:r0 + 128, tb * 512:(tb + 1) * 512], s_[:], reads=[ts_])
                    flush()
                k.dma("sp", send_loc, fap(send[:], [[1, 4096]]), reads=[tsend])
                k.barrier()
            if STOP == "A":
                break
            if NP > 1:
                groups = [[2 * i, 2 * i + 1] for i in range(4)] if NP == 2 else None
                for (a_, b_) in ((kx_loc, kx_all), (vx_loc, vx_all), (send_loc, send_all)):
                    nc.gpsimd.collective_compute("AllGather", ALU.bypass, replica_groups=groups,
                                                 ins=[a_], outs=[b_]).then_inc(cc_sem)
                    cc_cnt[0] += 1
                for eng in k.E.values():
                    eng.wait_ge(cc_sem, cc_cnt[0])

            tSst = T("Sst")
            with ExitStack() as spl:
                Sacc = [sb(spl, "Sacc%d" % i, [128, 8, 256]) for i in range(2)]
                ut = [[sb(spl, "ut%d_%d" % (d_, i), [128, 8, 256]) for i in range(2)] for d_ in range(2)]
                s16 = [[sb(spl, "s16%d_%d" % (d_, i), [128, 8, 256], BF16) for i in range(2)] for d_ in range(2)]
                tSacc = [T(), T()]
                tut = [[T(), T()], [T(), T()]]
                ts16 = [[T(), T()], [T(), T()]]
                for dr in range(2):
                    sel = SELF if dr == 0 else SELB
                    rk_ = 0 if dr == 0 else NP - 1
                    k.dma("sp", fap(ut[dr][1][:], [[1, 2048]]), send_all[rk_ * 128:(rk_ + 1) * 128, dr * 2048:(dr + 1) * 2048],
                          writes=[tut[dr][1]])
                    k.op("dve", lambda e: e.tensor_scalar(out=fap(Sacc[dr][:], [[1, 2048]]), in0=fap(ut[dr][1][:], [[1, 2048]]), scalar1=sel,
                                                           scalar2=0.0, op0=ALU.mult, op1=ALU.add),
                         reads=[tut[dr][1], tcst], writes=[tSacc[dr]])
                for step in range(NT):
                    for dr in range(2):
                        j = step if dr == 0 else NT - 1 - step
                        bi_ = step % 2
                        u_, tu_ = ut[dr][bi_], tut[dr][bi_]
                        s_, ts_ = s16[dr][bi_], ts16[dr][bi_]
                        k.dma("sp", fap(u_[:], [[1, 2048]]), U_d[j * 128:(j + 1) * 128, dr * 2048:(dr + 1) * 2048], writes=[tu_])
                        k.op("act", lambda e: e.activation(out=fap(s_[:], [[1, 2048]]), in_=fap(Sacc[dr][:], [[1, 2048]]), func=AF.Copy),
                             reads=[tSacc[dr]], writes=[ts_])
                        k.dma("sp", Sst[j * 128:(j + 1) * 128, dr * 2048:(dr + 1) * 2048], fap(s_[:], [[1, 2048]]),
                              reads=[ts_], writes=[tSst])
                        for h in range(8):
                            k.op("dve", lambda e: e.scalar_tensor_tensor(out=Sacc[dr][:, h, :], in0=Sacc[dr][:, h, :],
                                                                          scalar=gC[:, dr * 8 + h:dr * 8 + h + 1], in1=u_[:, h, :],
                                                                          op0=ALU.mult, op1=ALU.add),
                                 reads=[tu_, tgC], writes=[tSacc[dr]])
                k.barrier()
            if STOP == "pro":
                break
            with ExitStack() as sbk:
                DT = sb(sbk, "DT", [128, 8, 128])
                xi = sb(sbk, "xi", [128, 2, 8, 128])
                tDT, txi = T(), T()
                for h in range(8):
                    k.op("act", lambda e: e.activation(out=xi[:, 0, h, :], in_=QP1, func=AF.Exp, scale=lg[:, h:h + 1]),
                         reads=[tcst, tlg], writes=[txi])
                    k.op("act", lambda e: e.activation(out=xi[:, 1, h, :], in_=CMQ, func=AF.Exp, scale=lg[:, 8 + h:9 + h]),
                         reads=[tcst, tlg], writes=[txi])
                    k.op("act", lambda e: e.activation(out=DT[:, h, :], in_=DPOS, func=AF.Exp, scale=lg[:, h:h + 1]),
                         reads=[tcst, tlg], writes=[tDT])
                    k.op("act", lambda e: e.activation(out=etmp[:], in_=DNEG, func=AF.Exp, scale=lg[:, 8 + h:9 + h]),
                         reads=[tcst, tlg], writes=[tetmp])
                    k.op("dve", lambda e: e.tensor_tensor(out=DT[:, h, :], in0=DT[:, h, :], in1=etmp[:], op=ALU.mult),
                         reads=[tetmp], writes=[tDT])
                mixT = sb(sbk, "mixT", [128, 32, 512], BF16)
                wo = [sb(sbk, "wo%d" % i, [128, 16, 512], BF16) for i in range(2)]
                v_sb = [sb(sbk, "v_sb%d" % i, [128, NKT * 128], BF16) for i in range(2)]
                kt_sb = [sb(sbk, "kt_sb%d" % i, [128, SF], BF16) for i in range(2)]
                q_sb = [sb(sbk, "q_sb%d" % i, [128, 4, 512], BF16) for i in range(2)]
                ag_sb = sb(sbk, "ag_sb", [128, 4, 512], BF16)
                pt = [sb(sbk, "pt%d" % i, [128, 512], BF16) for i in range(6)]
                sacc = [sb(sbk, "sacc%d" % i, [128, 512]) for i in range(2)]
                sacc16 = [sb(sbk, "sacc16%d" % i, [128, 512], BF16) for i in range(2)]
                tsacc = [T(), T()]
                tsacc16 = [T(), T()]
                rc = [sb(sbk, "rc%d" % i, [128, 512]) for i in range(2)]
                o32 = [sb(sbk, "o32%d" % i, [128, 512]) for i in range(2)]
                rq_sb = [sb(sbk, "rq_sb%d" % i, [128, 8, 128], BF16) for i in range(2)]
                rk_sb = [sb(sbk, "rk_sb%d" % i, [128, 8, 128], BF16) for i in range(2)]
                ret_i = [0]
                rv_sb = sb(sbk, "rv_sb", [128, 2048], BF16)
                rg_sb = sb(sbk, "rg_sb", [128, 2048], BF16)
                S_sb_t = sb(sbk, "S_sb", [128, 4096], BF16)
                rv_sb_t = rv_sb
                rg_sb_t = rg_sb
                S_sb, rv_sb, rg_sb = S_sb_t[:], rv_sb_t[:], rg_sb_t[:]
                ptr = [[sb(sbk, "ptr%d_%d" % (p_, i), [128, 128], BF16) for i in range(8)] for p_ in range(2)]
                qx = [[sb(sbk, "qx%d_%d" % (p_, i), [128, 2, 128], BF16) for i in range(8)] for p_ in range(2)]
                r16 = [sb(sbk, "r16%d" % i, [128, 2048], BF16) for i in range(2)]
                junk = sb(sbk, "junk", [128, 256], BF16)
                ssq8 = sb(sbk, "ssq8", [128, 8])
                xin = [sb(sbk, "xin%d" % i, [128, 512]) for i in range(2)]
                xo = [sb(sbk, "xo%d" % i, [128, 512]) for i in range(2)]
                tmix, tag, trv, trg, tS, tjunk, tssq8 = [T() for _ in range(7)]
                tq = [T(), T()]
                trq = [T(), T()]
                trk = [T(), T()]
                tr16 = [T(), T()]
                trc = [T(), T()]
                to32 = [T(), T()]
                two = [T(), T()]
                tv = [T(), T()]
                tkt = [T(), T()]
                if SF == 4096:
                    S_alt, tS_alt = kt_sb[0][:], tkt[0]
                    rv_alt, trv_alt = v_sb[0][:, 0:2048], tv[0]
                    rg_alt, trg_alt = v_sb[0][:, 2048:4096], tv[0]
                else:
                    S_alt, tS_alt = sb(sbk, "S_alt", [128, 4096], BF16)[:], T()
                    rv_alt, trv_alt = sb(sbk, "rv_alt", [128, 2048], BF16)[:], T()
                    rg_alt, trg_alt = sb(sbk, "rg_alt", [128, 2048], BF16)[:], T()
                tpt = [T() for _ in range(6)]
                tptr = [[T() for _ in range(8)] for _ in range(2)]
                tqx = [[T() for _ in range(8)] for _ in range(2)]
                txin = [T(), T()]
                txo = [T(), T()]

                att_i = [0]
                kv_i = [0]
                wo_i = [0]
                woitems = [(cb_, half_) for _qb in range(NTB) for cb_ in range(8) for half_ in range(2)]

                def emit_woload(n_):
                    if n_ < len(woitems):
                        cb_, half_ = woitems[n_]
                        pump(1)
                        need_cast(two_[l][cb_])
                        k.dma("sp", wo[n_ % 2][:], wobs[l][cb_ * 128:(cb_ + 1) * 128, half_ * 8192:(half_ + 1) * 8192]
                              .rearrange("p (c n) -> p c n", n=512), reads=[two_[l][cb_]], writes=[two[n_ % 2]])

                emit_woload(0)
                emit_woload(1)
                def att_kv_loads(qb_, g_):
                    kvi_ = g_ % 2
                    for r in range(NP):
                        k.dma("sp", kt_sb[kvi_][:, r * TOK:(r + 1) * TOK], kx_all[r * 512 + g_ * 128: r * 512 + (g_ + 1) * 128, :],
                              writes=[tkt[kvi_]])
                        k.dma("sp", v_sb[kvi_][:, r * NT * 128:(r + 1) * NT * 128],
                              vx_all[r * 512 + g_ * 128: r * 512 + (g_ + 1) * 128, :], writes=[tv[kvi_]])

                def att_q_load(qb_, g_):
                    k.dma("sp", q_sb[g_ % 2][:], qT[g_ * 512:(g_ + 1) * 512, qb_ * 512:(qb_ + 1) * 512].rearrange("(h d) t -> d h t", d=128),
                          writes=[tq[g_ % 2]])

                def att_ag_load(qb_, g_):
                    k.dma("sp", ag_sb[:], agT[g_ * 512:(g_ + 1) * 512, qb_ * 512:(qb_ + 1) * 512].rearrange("(h d) t -> d h t", d=128),
                          writes=[tag])

                for qb in range(NTB):
                    for g in range(4):
                        kvi = g % 2
                        q_c, tq_c = q_sb[g % 2], tq[g % 2]
                        if qb == 0 and g == 0:
                            att_kv_loads(0, 0)
                            att_q_load(0, 0)
                            att_ag_load(0, 0)
                        if g < 3:
                            att_kv_loads(qb, g + 1)
                            att_q_load(qb, g + 1)
                        elif qb + 1 < NTB:
                            att_q_load(qb + 1, 0)
                        for hp in range(2):
                            items = [(kt_, hh) for kt_ in range(NKT) for hh in range(2)]

                            def pv(item, pi):
                                kt_, hh = item
                                k.op("pe", lambda e: e.matmul(ps[:, 2 + hh, :], v_sb[kvi][:, kt_ * 128:(kt_ + 1) * 128], pt[pi][:],
                                                               start=(kt_ == 0), stop=(kt_ == NKT - 1)),
                                     reads=[tv[kvi], tpt[pi]], writes=[tpb[2 + hh]], inc=True)
                                if kt_ % 3 == 0:
                                    k.op("pe", lambda e: e.matmul(ps[:, 4 + hh, :], ones16[:], pt[pi][:],
                                                                   start=(kt_ == 0), stop=False),
                                         reads=[tones, tpt[pi]], writes=[tpb[4 + hh]], inc=True)
                                elif kt_ == 1:
                                    k.op("dve", lambda e: e.tensor_copy(out=sacc[hh][:], in_=pt[pi][:]),
                                         reads=[tpt[pi]], writes=[tsacc[hh]])
                                else:
                                    k.op("dve", lambda e: e.tensor_tensor(out=sacc[hh][:], in0=sacc[hh][:], in1=pt[pi][:], op=ALU.add),
                                         reads=[tpt[pi], tsacc[hh]], writes=[tsacc[hh]])

                            LA = 3
                            sbanks = [0, 1, 6, 7]
                            pend = []
                            for item in items:
                                kt_, hh = item
                                hq = hp * 2 + hh
                                ai = att_i[0]
                                att_i[0] += 1
                                sbank = sbanks[ai % 4]
                                pi = ai % 6
                                k.op("pe", lambda e: e.matmul(ps[:, sbank, :], kt_sb[kvi][:, kt_ * 128:(kt_ + 1) * 128],
                                                               q_c[:, hq, :], start=True, stop=True),
                                     reads=[tkt[kvi], tq_c], writes=[tpb[sbank]], inc=True)
                                k.op("act", lambda e: e.activation(out=pt[pi][:], in_=ps[:, sbank, :], func=AF.Exp),
                                     reads=[tpb[sbank]], writes=[tpt[pi]])
                                pend.append((item, pi))
                                if len(pend) > LA:
                                    pv(*pend.pop(0))
                            while pend:
                                pv(*pend.pop(0))
                            for hh in range(2):
                                k.op("dve", lambda e: e.tensor_copy(out=sacc16[hh][:], in_=sacc[hh][:]), reads=[tsacc[hh]], writes=[tsacc16[hh]])
                                k.op("pe", lambda e: e.matmul(ps[:, 4 + hh, :], ones16[:], sacc16[hh][:], start=False, stop=True),
                                     reads=[tones, tsacc16[hh]], writes=[tpb[4 + hh]], inc=True)
                            for hh in range(2):
                                hq = hp * 2 + hh
                                k.op("act", lambda e: e.activation(out=rc[hh][:], in_=ps[:, 4 + hh, :], func=AF.Copy),
                                     reads=[tpb[4 + hh]], writes=[trc[hh]])
                                k.op("act", lambda e: e.activation(out=o32[hh][:], in_=ps[:, 2 + hh, :], func=AF.Copy),
                                     reads=[tpb[2 + hh]], writes=[to32[hh]])
                            for hh in range(2):
                                hq = hp * 2 + hh
                                k.op("dve", lambda e: e.reciprocal(out=rc[hh][:], in_=rc[hh][:]), reads=[trc[hh]], writes=[trc[hh]])
                                k.op("dve", lambda e: e.tensor_tensor(out=o32[hh][:], in0=o32[hh][:], in1=rc[hh][:], op=ALU.mult),
                                     reads=[to32[hh], trc[hh]], writes=[to32[hh]])
                                k.op("dve", lambda e: e.tensor_tensor(out=mixT[:, g * 4 + hq, :], in0=o32[hh][:], in1=ag_sb[:, hq, :],
                                                                       op=ALU.mult), reads=[to32[hh], tag], writes=[tmix])
                        if g < 3:
                            att_ag_load(qb, g + 1)
                        elif qb + 1 < NTB:
                            att_ag_load(qb + 1, 0)
                    ret_def = []

                    def ret_bufs(cj):
                        rb = cj % 2
                        if cj % 2 == 0:
                            return (rq_sb[rb], rk_sb[rb], r16[rb], trq[rb], trk[rb], tr16[rb], S_sb, tS, rv_sb, trv, rg_sb, trg)
                        return (rq_sb[rb], rk_sb[rb], r16[rb], trq[rb], trk[rb], tr16[rb], S_alt, tS_alt, rv_alt, trv_alt, rg_alt, trg_alt)

                    def ret_front(cj):
                        j = qb * 4 + cj
                        rq_c, rk_c, r16_c, trq_c, trk_c, tr16_c, S_c, tS_c, rv_c, trv_c, rg_c, trg_c = ret_bufs(cj)
                        pq = cj % 2
                        k.dma("sp", rq_c[:], rqT[:, j * 128:(j + 1) * 128].rearrange("(h d) t -> d h t", d=128), writes=[trq_c])
                        k.dma("sp", rk_c[:], rkT[:, j * 128:(j + 1) * 128].rearrange("(h d) t -> d h t", d=128), writes=[trk_c])
                        k.dma("sp", rv_c, rv_d[j * 128:(j + 1) * 128, :], writes=[trv_c])
                        k.dma("sp", rg_c, rg_d[j * 128:(j + 1) * 128, :], writes=[trg_c])
                        k.dma("sp", S_c, Sst[j * 128:(j + 1) * 128, :], reads=[tSst], writes=[tS_c])
                        for h in range(8):
                            sbank = 4 + h // 4
                            k.op("pe", lambda e: e.matmul(ps[:, sbank, (h % 4) * 128:(h % 4 + 1) * 128], rk_c[:, h, :], rq_c[:, h, :],
                                                           start=True, stop=True), reads=[trk_c, trq_c], writes=[tpb[sbank]], inc=(h % 4 == 3))
                        for h in range(8):
                            sbank = 4 + h // 4
                            k.op("dve", lambda e: e.tensor_tensor(out=ptr[pq][h][:], in0=ps[:, sbank, (h % 4) * 128:(h % 4 + 1) * 128],
                                                                   in1=DT[:, h, :], op=ALU.mult),
                                 reads=[tpb[sbank], tDT], writes=[tptr[pq][h]])
                            k.op("dve", lambda e: e.tensor_tensor(out=qx[pq][h][:], in0=fap(rq_c[:, h, :], [[0, 2], [1, 128]]),
                                                                    in1=fap(xi[:, 0, h, :], [[1024, 2], [1, 128]]), op=ALU.mult),
                                 reads=[trq_c, txi], writes=[tqx[pq][h]])

                    ret_front(0)
                    for cj in range(4):
                        j = qb * 4 + cj
                        rq_c, rk_c, r16_c, trq_c, trk_c, tr16_c, S_c, tS_c, rv_c, trv_c, rg_c, trg_c = ret_bufs(cj)
                        pq = cj % 2
                        for h in range(8):
                            bank = h // 2
                            oc = (h % 2) * 256
                            k.op("pe", lambda e: e.matmul(ps[:, bank, oc:oc + 256], ptr[pq][h][:], rv_c[:, h * 256:(h + 1) * 256],
                                                           start=True, stop=False), reads=[tptr[pq][h], trv_c], writes=[tpb[bank]], inc=False)
                            k.op("pe", lambda e: e.matmul(ps[:, bank, oc:oc + 256], qx[pq][h][:, 0, :], S_c[:, h * 256:(h + 1) * 256],
                                                           start=False, stop=False), reads=[tqx[pq][h], tS_c], writes=[tpb[bank]], inc=False)
                            k.op("pe", lambda e: e.matmul(ps[:, bank, oc:oc + 256], qx[pq][h][:, 1, :], S_c[:, (8 + h) * 256:(9 + h) * 256],
                                                           start=False, stop=True), reads=[tqx[pq][h], tS_c], writes=[tpb[bank]], inc=(h % 2 == 1))
                        if cj + 1 < 4:
                            ret_front(cj + 1)
                        for f_ in ret_def:
                            f_()
                        del ret_def[:]
                        for h in range(8):
                            bank = h // 2
                            oc = (h % 2) * 256
                            k.op("act", lambda e: e.activation(out=junk[:], in_=ps[:, bank, oc:oc + 256], func=AF.Square,
                                                                accum_out=ssq8[:, h:h + 1]),
                                 reads=[tpb[bank]], writes=[tjunk, tssq8])
                        k.op("act", lambda e: e.activation(out=ssq8[:], in_=ssq8[:], func=AF.Ln, scale=1.0 / 256, bias=EPSC),
                             reads=[tssq8, tcst], writes=[tssq8])
                        k.op("act", lambda e: e.activation(out=ssq8[:], in_=ssq8[:], func=AF.Exp, scale=-0.5), reads=[tssq8], writes=[tssq8])
                        for h in range(8):
                            bank = h // 2
                            oc = (h % 2) * 256
                            k.op("dve", lambda e: e.scalar_tensor_tensor(out=r16_c[:, h * 256:(h + 1) * 256], in0=ps[:, bank, oc:oc + 256],
                                                                          scalar=ssq8[:, h:h + 1], in1=rg_c[:, h * 256:(h + 1) * 256],
                                                                          op0=ALU.mult, op1=ALU.mult),
                                 reads=[tpb[bank], tssq8, trg_c], writes=[tr16_c])

                        def tr_part(r16_c=r16_c, tr16_c=tr16_c, cj=cj):
                            for g4 in range(4):
                                tbk = 6 + g4 % 2
                                for c4 in range(4):
                                    c = g4 * 4 + c4
                                    k.op("pe", lambda e: e.matmul(ps[:, tbk, c4 * 128:(c4 + 1) * 128], r16_c[:, c * 128:(c + 1) * 128], ident16[:],
                                                                   start=True, stop=True), reads=[tr16_c, tid], writes=[tpb[tbk]], inc=(c4 == 3))
                                k.op("act", lambda e: e.activation(out=mixT[:, 16 + g4 * 4:16 + (g4 + 1) * 4, cj * 128:(cj + 1) * 128],
                                                                    in_=fap(ps[:, tbk, :], [[128, 4], [1, 128]]), func=AF.Copy),
                                     reads=[tpb[tbk]], writes=[tmix])
                        ret_def.append(tr_part)
                    for f_ in ret_def:
                        f_()
                    del ret_def[:]
                    if qb + 1 < NTB:
                        att_kv_loads(qb + 1, 0)
                    for cb in range(8):
                        wrow = cb * 128
                        wob = wobs[l]
                        banks = [4, 5, 6, 7] if cb % 2 == 0 else [0, 1, 2, 3]
                        for half in range(2):
                            wi = wo_i[0] % 2
                            for tt in range(4):
                                for c in range(16):
                                    cg = half * 16 + c
                                    k.op("pe", lambda e: e.matmul(ps[:, banks[tt], :], mixT[:, cg, tt * 128:(tt + 1) * 128],
                                                                   wo[wi][:, c, :], start=(cg == 0), stop=(cg == 31)),
                                         reads=[tmix, two[wi]], writes=[tpb[banks[tt]]], inc=(c == 15))
                            emit_woload(wo_i[0] + 2)
                            wo_i[0] += 1
                        for tt in range(4):
                            j = qb * 4 + tt
                            xi_ = (cb * 4 + tt) % 2
                            k.dma("sp", xin[xi_][:], xsrc[j * 128:(j + 1) * 128, cb * 512:(cb + 1) * 512], writes=[txin[xi_]])
                            k.op("dve", lambda e: e.tensor_tensor(out=xo[xi_][:], in0=ps[:, banks[tt], :], in1=xin[xi_][:], op=ALU.add),
                                 reads=[tpb[banks[tt]], txin[xi_]], writes=[txo[xi_]])
                            k.dma("sp", xres[j * 128:(j + 1) * 128, cb * 512:(cb + 1) * 512], xo[xi_][:], reads=[txo[xi_]])
                k.barrier()
        with ExitStack() as sf:
            fxt = [sb(sf, "fxt%d" % i, [128, D]) for i in range(2)]
            fyo = [sb(sf, "fyo%d" % i, [128, D]) for i in range(2)]
            fw_ = sb(sf, "ffw", [128, D])
            fss = [sb(sf, "fss%d" % i, [128, 1]) for i in range(2)]
            fhj = [sb(sf, "fjunk%d" % i, [128, D], BF16) for i in range(2)]
            tfxt, tfyo, tfss, tfhj = [[T(), T()] for _ in range(4)]
            tfw = T()
            k.dma("sp", fw_[:], bass.AP(final_norm, 0, [[0, 128], [1, D]]), writes=[tfw])
            for j in range(NT):
                i_ = j % 2
                k.dma("sp", fxt[i_][:], xres[j * 128:(j + 1) * 128, :], writes=[tfxt[i_]])
                k.op("act", lambda e: e.activation(out=fhj[i_][:], in_=fxt[i_][:], func=AF.Square, accum_out=fss[i_][:, 0:1]),
                     reads=[tfxt[i_]], writes=[tfhj[i_], tfss[i_]])
                k.op("act", lambda e: e.activation(out=fss[i_][:, 0:1], in_=fss[i_][:, 0:1], func=AF.Ln, scale=1.0 / D, bias=EPSC),
                     reads=[tfss[i_], tcst], writes=[tfss[i_]])
                k.op("act", lambda e: e.activation(out=fss[i_][:, 0:1], in_=fss[i_][:, 0:1], func=AF.Exp, scale=-0.5),
                     reads=[tfss[i_]], writes=[tfss[i_]])
                k.op("dve", lambda e: e.scalar_tensor_tensor(out=fyo[i_][:], in0=fxt[i_][:], scalar=fss[i_][:, 0:1], in1=fw_[:],
                                                              op0=ALU.mult, op1=ALU.mult), reads=[tfxt[i_], tfss[i_], tfw], writes=[tfyo[i_]])
                k.dma("sp", out_d[j * 128:(j + 1) * 128, :], fyo[i_][:], reads=[tfyo[i_]])
            k.barrier()
        print("kernel instructions:", k.n_inst)
    return nc


def make_consts(TOK, tok0, sel_f, sel_b):
    t = np.arange(tok0, tok0 + TOK)
    row = (t // 64).astype(np.float32)
    col = (t % 64).astype(np.float32)
    inv = (np.float32(10000.0) ** (-np.arange(0, 64, 2, dtype=np.float32) / np.float32(64))).astype(np.float32)
    ar = row[:, None] * inv[None, :]
    ac = col[:, None] * inv[None, :]
    cr, sr, cc_, sc = np.cos(ar), np.sin(ar), np.cos(ac), np.sin(ac)
    rope = np.concatenate([cr, cr, cc_, cc_, -sr, sr, -sc, sc], axis=1).astype(np.float32)
    cst = np.zeros((128, 648), np.float32)
    cst[:, 0:128] = np.eye(128, dtype=np.float32)
    q = np.arange(128, dtype=np.float32)
    cst[:, 128:256] = (q + 1)[None, :]
    cst[:, 256:384] = (128 - q)[None, :]
    kk = q[:, None]
    cst[:, 384:512] = np.maximum(q[None, :] - kk, 0)
    cst[:, 512:640] = np.maximum(kk - q[None, :], 0)
    cst[:, 640] = 127 - q
    cst[:, 641] = q
    cst[:, 642] = sel_f
    cst[:, 643] = sel_b
    cst[:, 644] = 1e-6
    return rope, cst


def run(inputs, L, NP, S, B):
    TOK = S // NP
    ncores = B * NP
    nc = build(L, TOK, NP)
    x = np.asarray(inputs["x"], np.float32)
    rdec = np.concatenate([np.asarray(inputs["ret_decay_fwd"], np.float32),
                           np.asarray(inputs["ret_decay_bwd"], np.float32)], axis=1)
    shared = {
        "w_in": np.ascontiguousarray(inputs["w_in"], dtype=np.float32),
        "w_out": np.ascontiguousarray(inputs["w_out"], dtype=np.float32),
        "norm_w": np.ascontiguousarray(inputs["norm_w"], dtype=np.float32),
        "q_norm": np.ascontiguousarray(inputs["q_norm"], dtype=np.float32),
        "k_norm": np.ascontiguousarray(inputs["k_norm"], dtype=np.float32),
        "rdec": np.ascontiguousarray(rdec),
        "ret_norm": np.ascontiguousarray(np.asarray(inputs["ret_norm"], np.float32).reshape(L, 2048)),
        "final_norm": np.ascontiguousarray(np.asarray(inputs["final_norm"], np.float32).reshape(1, D)),
    }
    in_maps = []
    for b in range(B):
        for r in range(NP):
            rope, cst = make_consts(TOK, r * TOK, 1.0 if (NP == 2 and r == 1) else 0.0, 1.0 if (NP == 2 and r == 0) else 0.0)
            m = dict(shared)
            m["x"] = np.ascontiguousarray(x[b, r * TOK:(r + 1) * TOK, :])
            m["rope"] = rope
            m["cst"] = cst
            in_maps.append(m)
    res = run_bass_kernel_spmd(nc, in_maps, core_ids=list(range(ncores)))
    out = np.zeros((B, S, D), np.float32)
    for b in range(B):
        for r in range(NP):
            out[b, r * TOK:(r + 1) * TOK, :] = res.results[b * NP + r]["out"]
    return out


NP_CFG = 2


def kernel(x, norm_w, w_in, q_norm, k_norm, ret_decay_fwd, ret_decay_bwd, ret_norm, w_out, final_norm):
    inputs = dict(x=x, norm_w=norm_w, w_in=w_in, q_norm=q_norm, k_norm=k_norm, ret_decay_fwd=ret_decay_fwd,
                  ret_decay_bwd=ret_decay_bwd, ret_norm=ret_norm, w_out=w_out, final_norm=final_norm)
    return run(inputs, L=4, NP=NP_CFG, S=4096, B=4)
```
